# Optimizing a Trainium2 kernel written in Bass

```python
import math
import jax, jax.numpy as jnp
from jax import lax
import numpy as np

D_MODEL = 1024
BATCH = 2
SEQ = 8192
DEPTH = 2

CTX_LEN = 256
GRID_W = 64
N_HEADS = 8
QK_DIM = 64
V_DIM = 2 * QK_DIM
ATTN_W = N_HEADS * V_DIM
CONV_W = 1024
CONV_K = 3
D_FF = -(-8 * D_MODEL // (3 * 256)) * 256
ROPE_BASE = 10000.0
ROPE_PAIRS_PER_AXIS = QK_DIM // 4
ATTN_SCALE = QK_DIM ** -0.5
Q_BLOCK = 128
EPS = 1e-6
N_MOD = 6

OFF_K = N_HEADS * 2 * QK_DIM
OFF_V = OFF_K + N_HEADS * 2 * QK_DIM
OFF_CX = OFF_V + ATTN_W
OFF_CB = OFF_CX + CONV_W
OFF_CC = OFF_CB + CONV_W
OFF_GA = OFF_CC + CONV_W
OFF_GC = OFF_GA + D_MODEL
N_IN = OFF_GC + D_MODEL
SPLITS = (OFF_K, OFF_V, OFF_CX, OFF_CB, OFF_CC, OFF_GA, OFF_GC)

kernel_name = "hybrid_diffattn_shortconv_dit"


def rms_norm(x, g):
    xf = x.astype(jnp.float32)
    y = xf * lax.rsqrt(jnp.mean(xf * xf, axis=-1, keepdims=True) + EPS)
    return (y * g.astype(jnp.float32)).astype(x.dtype)


def modulate(h, shift, scale):
    return h * (1 + scale) + shift


def axial_rope(n_tokens):
    rows = n_tokens // GRID_W
    row = jnp.repeat(jnp.arange(rows, dtype=jnp.float32), GRID_W)
    col = jnp.tile(jnp.arange(GRID_W, dtype=jnp.float32), rows)
    inv_freq = ROPE_BASE ** (-jnp.arange(ROPE_PAIRS_PER_AXIS, dtype=jnp.float32) / ROPE_PAIRS_PER_AXIS)
    ang = jnp.concatenate([row[:, None] * inv_freq, col[:, None] * inv_freq], axis=-1)
    return jnp.cos(ang), jnp.sin(ang)


def apply_rope(t, cos, sin):
    t1, t2 = jnp.split(t, 2, axis=-1)
    cos = cos.astype(t.dtype)
    sin = sin.astype(t.dtype)
    return jnp.concatenate([t1 * cos - t2 * sin, t2 * cos + t1 * sin], axis=-1)


def split_qk_heads(t, g):
    b, n, _ = t.shape
    t = t.reshape(b, n, N_HEADS, 2, QK_DIM).transpose(0, 2, 3, 1, 4)
    return rms_norm(t, g)


def split_v_heads(v):
    b, n, _ = v.shape
    return v.reshape(b, n, N_HEADS, V_DIM).transpose(0, 2, 1, 3)


def diff_weights(s, lam):
    p = jax.nn.softmax(s.astype(jnp.float32), axis=-1)
    return p[:, :, 0] - lam * p[:, :, 1]


def diff_attn_latent(q, k_all, v_all, lam):
    b, h, _, n, d = q.shape
    nb = n // Q_BLOCK
    qb = q.reshape(b, h, 2, nb, Q_BLOCK, d).transpose(3, 0, 1, 2, 4, 5)

    def one_block(qblk):
        s = jnp.einsum("bhmqd,bhmkd->bhmqk", qblk, k_all) * ATTN_SCALE
        w = diff_weights(s, lam)
        return jnp.einsum("bhqk,bhkd->bhqd", w.astype(v_all.dtype), v_all)

    o = lax.map(one_block, qb)
    return o.transpose(1, 2, 0, 3, 4).reshape(b, h, n, V_DIM)


def diff_attn_ctx(q, k, v, lam):
    s = jnp.einsum("bhmqd,bhmkd->bhmqk", q, k) * ATTN_SCALE
    w = diff_weights(s, lam)
    return jnp.einsum("bhqk,bhkd->bhqd", w.astype(v.dtype), v)


def diff_head_out(o, g, lam_init):
    o = rms_norm(o, g) * (1 - lam_init)
    b, h, n, d = o.shape
    return o.transpose(0, 2, 1, 3).reshape(b, n, h * d)


def short_conv(u, w):
    up = jnp.pad(u, ((0, 0), (1, 1), (0, 0)))
    return up[:, :-2] * w[0] + up[:, 1:-1] * w[1] + up[:, 2:] * w[2]


def merge_branches(attn, cx, cb, cc, ga, gc, conv_w, w_pa, w_pc, w_o):
    y_conv = cb * short_conv(cc * cx, conv_w)
    ya = attn @ w_pa
    yc = y_conv @ w_pc
    return (jax.nn.sigmoid(ga) * ya + jax.nn.sigmoid(gc) * yc) @ w_o


def swiglu(h, wg, wu, wd):
    return (jax.nn.silu(h @ wg) * (h @ wu)) @ wd


def setup_inputs(seed: int = 0) -> dict:
    key = jax.random.key(seed)
    ks = jax.random.split(key, 24)
    f32 = jnp.float32
    nrm = lambda k, shape, s: jax.random.normal(k, shape, f32) * s
    return {
        "x": nrm(ks[0], (BATCH, SEQ, D_MODEL), 1.0),
        "c": nrm(ks[1], (BATCH, D_MODEL), 1.0),
        "ctx": nrm(ks[2], (BATCH, CTX_LEN, D_MODEL), 1.0),
        "c_ctx": nrm(ks[3], (D_MODEL,), 1.0),
        "w_ada": nrm(ks[4], (DEPTH, D_MODEL, N_MOD * D_MODEL), 0.5 * D_MODEL ** -0.5),
        "b_ada": nrm(ks[5], (DEPTH, N_MOD * D_MODEL), 0.02),
        "norm1_g": 1.0 + nrm(ks[6], (DEPTH, D_MODEL), 0.02),
        "norm2_g": 1.0 + nrm(ks[7], (DEPTH, D_MODEL), 0.02),
        "w_in": nrm(ks[8], (DEPTH, D_MODEL, N_IN), D_MODEL ** -0.5),
        "q_norm_g": 1.0 + nrm(ks[9], (DEPTH, QK_DIM), 0.02),
        "k_norm_g": 1.0 + nrm(ks[10], (DEPTH, QK_DIM), 0.02),
        "lambda_q1": nrm(ks[11], (DEPTH, QK_DIM), 0.1),
        "lambda_k1": nrm(ks[12], (DEPTH, QK_DIM), 0.1),
        "lambda_q2": nrm(ks[13], (DEPTH, QK_DIM), 0.1),
        "lambda_k2": nrm(ks[14], (DEPTH, QK_DIM), 0.1),
        "subln_g": 1.0 + nrm(ks[15], (DEPTH, V_DIM), 0.02),
        "conv_w": nrm(ks[16], (DEPTH, CONV_K, CONV_W), CONV_K ** -0.5),
        "w_pa": nrm(ks[17], (DEPTH, ATTN_W, D_MODEL), ATTN_W ** -0.5),
        "w_pc": nrm(ks[18], (DEPTH, CONV_W, D_MODEL), CONV_W ** -0.5),
        "w_o": nrm(ks[19], (DEPTH, D_MODEL, D_MODEL), D_MODEL ** -0.5),
        "w_ffn_gate": nrm(ks[20], (DEPTH, D_MODEL, D_FF), D_MODEL ** -0.5),
        "w_ffn_up": nrm(ks[21], (DEPTH, D_MODEL, D_FF), D_MODEL ** -0.5),
        "w_ffn_down": nrm(ks[22], (DEPTH, D_FF, D_MODEL), D_FF ** -0.5),
    }


def reference(x, c, ctx, c_ctx, w_ada, b_ada, norm1_g, norm2_g, w_in, q_norm_g, k_norm_g,
              lambda_q1, lambda_k1, lambda_q2, lambda_k2, subln_g, conv_w, w_pa, w_pc, w_o,
              w_ffn_gate, w_ffn_up, w_ffn_down):
    n_lat = x.shape[1]
    cos, sin = axial_rope(n_lat)
    for li in range(DEPTH):
        last = li == DEPTH - 1
        lam_init = 0.8 - 0.6 * math.exp(-0.3 * li)
        lam = (jnp.exp(jnp.sum((lambda_q1[li] * lambda_k1[li]).astype(jnp.float32)))
               - jnp.exp(jnp.sum((lambda_q2[li] * lambda_k2[li]).astype(jnp.float32)))
               + lam_init)

        mod = jax.nn.silu(c) @ w_ada[li] + b_ada[li]
        mod_c = jax.nn.silu(c_ctx) @ w_ada[li] + b_ada[li]
        sh1, sc1, g1, sh2, sc2, g2 = [m[:, None, :] for m in jnp.split(mod, N_MOD, axis=-1)]
        csh1, csc1, cg1, csh2, csc2, cg2 = jnp.split(mod_c, N_MOD, axis=-1)

        h = modulate(rms_norm(x, norm1_g[li]), sh1, sc1)
        hc = modulate(rms_norm(ctx, norm1_g[li]), csh1, csc1)
        q, k, v, cx, cb, cc, ga, gc = jnp.split(h @ w_in[li], SPLITS, axis=-1)
        if last:
            kc, vc = jnp.split(hc @ w_in[li][:, OFF_K:OFF_CX], [OFF_V - OFF_K], axis=-1)
        else:
            qc, kc, vc, cxc, cbc, ccc, gac, gcc = jnp.split(hc @ w_in[li], SPLITS, axis=-1)

        kc_h = split_qk_heads(kc, k_norm_g[li])
        vc_h = split_v_heads(vc)
        q_h = apply_rope(split_qk_heads(q, q_norm_g[li]), cos, sin)
        k_h = apply_rope(split_qk_heads(k, k_norm_g[li]), cos, sin)
        k_all = jnp.concatenate([kc_h, k_h], axis=3)
        v_all = jnp.concatenate([vc_h, split_v_heads(v)], axis=2)
        attn = diff_head_out(diff_attn_latent(q_h, k_all, v_all, lam), subln_g[li], lam_init)

        mix = merge_branches(attn, cx, cb, cc, ga, gc, conv_w[li], w_pa[li], w_pc[li], w_o[li])
        x = x + g1 * mix
        hf = modulate(rms_norm(x, norm2_g[li]), sh2, sc2)
        x = x + g2 * swiglu(hf, w_ffn_gate[li], w_ffn_up[li], w_ffn_down[li])

        if not last:
            qc_h = split_qk_heads(qc, q_norm_g[li])
            attn_c = diff_head_out(diff_attn_ctx(qc_h, kc_h, vc_h, lam), subln_g[li], lam_init)
            mix_c = merge_branches(attn_c, cxc, cbc, ccc, gac, gcc, conv_w[li],
                                   w_pa[li], w_pc[li], w_o[li])
            ctx = ctx + cg1 * mix_c
            hfc = modulate(rms_norm(ctx, norm2_g[li]), csh2, csc2)
            ctx = ctx + cg2 * swiglu(hfc, w_ffn_gate[li], w_ffn_up[li], w_ffn_down[li])
    return x
```

```python
import math
import os
from contextlib import ExitStack
import numpy as np
import concourse.bass as bass
import concourse.mybir as mybir
from concourse.bass_utils import run_bass_kernel_spmd

F32, BF16 = mybir.dt.float32, mybir.dt.bfloat16
ALU = mybir.AluOpType
AF = mybir.ActivationFunctionType

D = 1024
KC = 8
NCTX = 256
H = 8
DFF = 2816
FC = 22
EPS = 1e-6
GRID_W = 64
OFF_K, OFF_V, OFF_CX, OFF_CB, OFF_CC, OFF_GA, OFF_GC = 1024, 2048, 3072, 4096, 5120, 6144, 7168
VL = 98
NV = 2 * VL + 34
MW = 48
ATTN_SCALE = 0.125


class Buf:
    __slots__ = ("w", "rs", "multi", "name", "excl")

    def __init__(self, name="", multi=False, excl=False):
        self.excl = excl
        self.w = {}
        self.rs = {}
        self.multi = multi
        self.name = name


def _add(d, tok):
    if tok is None:
        return
    k = id(tok[0])
    if k not in d or d[k][1] < tok[1]:
        d[k] = tok


class Sched:
    ENG = ("pe", "act", "dve", "pool", "sp")
    LIMIT = 20000

    def __init__(self, nc):
        self.nc = nc
        self.e = {"pe": nc.tensor, "act": nc.scalar, "dve": nc.vector, "pool": nc.gpsimd, "sp": nc.sync}
        self.cur = {}
        self.nsem = 0
        for en in ("pe", "act", "dve", "pool"):
            self.cur[en] = [self._newsem(en), 0]
        self.seen = {en: {} for en in self.ENG}
        self.dsem = {}
        self.latest = {}

    def _newsem(self, name):
        self.nsem += 1
        return self.nc.alloc_semaphore(f"s_{name}_{self.nsem}")

    def _waits(self, eng, reads, writes):
        need = {}
        for b in reads:
            for t in b.w.values():
                _add(need, t)
            if b.excl:
                for t in b.rs.values():
                    _add(need, t)
        for b in writes:
            if not b.multi:
                for t in b.w.values():
                    _add(need, t)
            for t in b.rs.values():
                _add(need, t)
        self._emit_waits(eng, need.values())

    def _emit_waits(self, eng, toks):
        seen = self.seen[eng]
        for sem, val in toks:
            if eng == "pe" and sem is self.cur["pe"][0]:
                continue
            k = id(sem)
            if seen.get(k, 0) >= val:
                continue
            self.e[eng].wait_ge(sem, val)
            seen[k] = val

    def _record(self, tok, reads, writes):
        self.latest[id(tok[0])] = tok
        for b in reads:
            if b.excl:
                b.w = {id(tok[0]): tok}
                b.rs = {}
            else:
                _add(b.rs, tok)
        for b in writes:
            if b.multi:
                _add(b.w, tok)
            else:
                b.w = {id(tok[0]): tok}
                b.rs = {}

    def op(self, eng, fn, reads=(), writes=()):
        self._waits(eng, reads, writes)
        inst = fn(self.e[eng])
        c = self.cur[eng]
        if c[1] >= self.LIMIT:
            c[0] = self._newsem(eng)
            c[1] = 0
        c[1] += 1
        inst.then_inc(c[0], 1)
        self._record((c[0], c[1]), reads, writes)

    def group(self, fns, reads=(), writes=()):
        self._waits("pe", reads, writes)
        inst = None
        for fn in fns:
            inst = fn(self.e["pe"])
        c = self.cur["pe"]
        if c[1] >= self.LIMIT:
            c[0] = self._newsem("pe")
            c[1] = 0
        c[1] += 1
        inst.then_inc(c[0], 1)
        self._record((c[0], c[1]), reads, writes)

    def dma(self, q, pairs, reads=(), writes=(), key=None):
        self._waits(q, reads, writes)
        if key not in self.dsem:
            self.dsem[key] = [self._newsem("d"), 0]
        c = self.dsem[key]
        if c[1] >= 16 * 1000:
            c[0] = self._newsem("d")
            c[1] = 0
        for out, in_ in pairs:
            self.e[q].dma_start(out=out, in_=in_).then_inc(c[0], 16)
            c[1] += 16
        self._record((c[0], c[1]), reads, writes)

    def collective(self, ins, outs, groups, reads, writes):
        self._waits("pool", reads, writes)
        sem = self._newsem("cc")
        self.e["pool"].collective_compute(
            "AllGather", ALU.bypass, replica_groups=groups, ins=ins, outs=outs
        ).then_inc(sem)
        self._record((sem, 1), reads, writes)

    def barrier(self):
        toks = list(self.latest.values())
        for en in self.ENG:
            self._emit_waits(en, toks)


def _fm(v):
    return np.ascontiguousarray(np.asarray(v, np.float32).reshape(-1, 128).T)


def _wch(w, cols):
    K = w.shape[0]
    wk = w.reshape(K // 128, 128, -1)
    return np.ascontiguousarray(wk[:, :, cols].transpose(1, 0, 2))


def _swap_cols(base):
    i = np.arange(128)
    return base + (i // 64) * 64 + ((i % 64) + 32) % 64


def _layer_weights(w_in, w_pa, w_pc, w_o, wg, wu, wd):
    wq = []
    for off in (OFF_K, 0):
        for h in range(H):
            base = off + 128 * h
            cols = np.concatenate([base + np.arange(128), _swap_cols(base)])
            wq.append(_wch(w_in, cols))
    wv = [_wch(w_in, OFF_V + 512 * g + np.arange(512)) for g in range(2)]
    wconv = [_wch(w_in, np.concatenate([o + 128 * m + np.arange(128) for o in (OFF_CX, OFF_CC, OFF_CB)]))
             for m in range(8)]
    wgt = [_wch(w_in, np.concatenate([o + 128 * m + np.arange(128) for o in (OFF_GA, OFF_GC)]))
           for m in range(8)]
    wp = [_wch(w, np.arange(1024)) for w in (w_pa, w_pc, w_o)]
    wff = [np.concatenate([_wch(wg, 128 * f + np.arange(128)), _wch(wu, 128 * f + np.arange(128))], axis=2)
           for f in range(FC)]
    wdn = [_wch(wd, 128 * m + np.arange(128)) for m in range(8)]
    st = lambda xs: np.ascontiguousarray(np.stack(xs))
    return dict(wq=st(wq), wv=st(wv), wconv=st(wconv), wgt=st(wgt), wp=st(wp), wff=st(wff),
                wdn=st(wdn))


def _rope_tabs(pos0, nlat):
    t = np.arange(pos0, pos0 + nlat)
    row = (t // GRID_W).astype(np.float32)
    col = (t % GRID_W).astype(np.float32)
    inv = (np.float32(10000.0) ** (-np.arange(16, dtype=np.float32) / np.float32(16))).astype(np.float32)
    ang = np.concatenate([row[:, None] * inv, col[:, None] * inv], axis=-1).astype(np.float32)
    cos, sin = np.cos(ang).astype(np.float32), np.sin(ang).astype(np.float32)
    T = nlat + NCTX
    tabs = np.zeros((128, 2, T), np.float32)
    tabs[:, 0, nlat:] = 1.0
    p = np.arange(128)
    tabs[:, 0, :nlat] = cos.T[p % 32]
    sgn = np.where((p % 64) < 32, -1.0, 1.0).astype(np.float32)
    tabs[:, 1, :nlat] = sin.T[p % 32] * sgn[:, None]
    return tabs


def build(NLAT, depth=2):
    T = NLAT + NCTX
    NKL = 4 * NLAT
    B = NLAT // 128
    HR = 32768 // NLAT
    R = 2048 + HR
    tiles = [(t0, 512, 0) for t0 in range(0, NLAT, 512)] + [(NLAT, 256, 1)]

    nc = bass.Bass("TRN2", target_bir_lowering=False)
    S = Sched(nc)
    ps_e = nc.tensor
    din = lambda name, shape: nc.dram_tensor(name, list(shape), F32, kind="ExternalInput").ap()
    xT_d = din("xT", [128, KC, T])
    tabs_d = din("tabs", [128, 2, T])
    vecs_d = din("vecs", [128, NV])
    W = []
    for l in range(depth):
        W.append(dict(
            wq=din(f"wq{l}", [16, 128, 8, 256]), wv=din(f"wv{l}", [2, 128, 8, 512]),
            wconv=din(f"wconv{l}", [8, 128, 8, 384]), wgt=din(f"wgt{l}", [8, 128, 8, 256]),
            wp=din(f"wp{l}", [3, 128, 8, 1024]), wff=din(f"wff{l}", [FC, 128, 8, 256]),
            wdn=din(f"wdn{l}", [8, 128, FC, 128]), wada=din(f"wada{l}", [12, 128, 8, 128])))
    yT_d = nc.dram_tensor("yT", [128, KC, NLAT], F32, kind="ExternalOutput").ap()
    NP_ = 4
    ownK_d = [[nc.dram_tensor(f"ownK{l}_{i}", [256, NLAT], BF16).ap() for i in range(NP_)] for l in range(depth)]
    gatK_d = [[nc.dram_tensor(f"gatK{l}_{i}", [1024, NLAT], BF16).ap() for i in range(NP_)] for l in range(depth)]
    ownV_d = [[nc.dram_tensor(f"ownV{l}_{i}", [256, NLAT], BF16).ap() for i in range(NP_)] for l in range(depth)]
    gatV_d = [[nc.dram_tensor(f"gatV{l}_{i}", [1024, NLAT], BF16).ap() for i in range(NP_)] for l in range(depth)]
    ownH_d = [nc.dram_tensor(f"ownH{l}", [HR, NLAT], BF16).ap() for l in range(depth)]
    gatH_d = [nc.dram_tensor(f"gatH{l}", [4 * HR, NLAT], BF16).ap() for l in range(depth)]
    ownM_d = nc.dram_tensor("ownM", [128, MW], F32).ap()
    gatM_d = nc.dram_tensor("gatM", [4 * 128, MW], F32).ap()
    QS = nc.dram_tensor("QS", [H, 128, T], BF16).ap()
    US = nc.dram_tensor("US", [8, 128, T], F32).ap()
    CBS = nc.dram_tensor("CBS", [8, 128, T], F32).ap()
    SGS = nc.dram_tensor("SGS", [8, 128, 2, T], F32).ap()
    ATS = nc.dram_tensor("ATS", [H, 128, T], BF16).ap()
    bQS, bUS, bCBS, bSGS, bATS = (Buf(n, multi=True) for n in ("QS", "US", "CBS", "SGS", "ATS"))
    bownK = [[Buf(f"ownK{l}{i}", multi=True) for i in range(4)] for l in range(depth)]
    bownV = [[Buf(f"ownV{l}{i}", multi=True) for i in range(4)] for l in range(depth)]
    bownH = [Buf(f"ownH{l}", multi=True) for l in range(depth)]
    bgatK = [[Buf(f"gatK{l}{i}") for i in range(4)] for l in range(depth)]
    bgatV = [[Buf(f"gatV{l}{i}") for i in range(4)] for l in range(depth)]
    bgatH = [Buf(f"gatH{l}") for l in range(depth)]

    sb = lambda name, shape, dt: nc.alloc_sbuf_tensor(name, list(shape), dt)
    _uid = [0]

    def sbt(name, shape, dt):
        _uid[0] += 1
        return nc.sbuf_tensor(f"{name}_u{_uid[0]}", list(shape), dt)
    xres = sb("xres", [128, KC, T], F32); bx = [[Buf(f"x{c}_{i}") for i in range(len(tiles))] for c in range(KC)]
    vec = sb("vec", [128, NV], F32); bvec = Buf("vec")
    onesD = sb("onesD", [128, 128], BF16)
    ones128 = sb("ones128", [128, 128], BF16)
    blk64 = sb("blk64", [128, 128], BF16)
    ones1 = sb("ones1", [128, 128], BF16)
    onesf = sb("onesf", [128, 128], F32)
    epsb = sb("epsb", [128, 1], F32)
    bconst = Buf("const")
    mods = sb("mods", [128, depth, 48, 2], F32); bmods = Buf("mods")
    a12 = sb("a12", [128, depth, 2, 8, 2], F32)
    lamv = sb("lamv", [128, depth], F32)
    sgl = sb("sgl", [128, depth], F32)
    ctxK = sb("ctxK", [128, H, NCTX], BF16); bctxK = Buf("ctxK", multi=True)
    ctxV = sb("ctxV", [128, 2, 1024], BF16); bctxV = Buf("ctxV", multi=True)
    halo = sb("halo", [128, 2, 8], F32); bhalo = Buf("halo")
    NWS = 3
    wsl = [None] * NWS
    bwsl = [Buf(f"wsl{i}") for i in range(NWS)]
    wctr = [0]

    def alloc_wsl(es):
        for i in range(NWS):
            wsl[i] = es.enter_context(sbt(f"wsl{i}", [128, 4096], BF16))
    ps = nc.alloc_psum_tensor("ps", [128, 8, 512], F32)
    bps = [Buf(f"ps{i}", excl=True) for i in range(8)]
    bkctr = [0]

    def nextbank():
        b = bkctr[0] % 8
        bkctr[0] += 1
        return b

    wplan = []
    wstate = {"next": 0, "issued": 0, "views": {}}

    def set_plan(items):
        wplan[:] = items
        wstate["next"] = 0
        wstate["issued"] = 0
        wstate["views"] = {}

    def _issue(k):
        src, kk, ncols = wplan[k]
        i = wctr[0] % NWS
        wctr[0] += 1
        view = wsl[i][:, 0:kk * ncols].rearrange("p (k n) -> p k n", k=kk)
        S.dma("pool", [(view, src)], writes=[bwsl[i]], key=f"wsl{i}")
        wstate["views"][k] = (view, bwsl[i])

    def load_w(src, kk, ncols):
        k = wstate["next"]
        wstate["next"] += 1
        assert wplan[k][1:] == (kk, ncols), (k, wplan[k][1:], kk, ncols)
        while wstate["issued"] < min(k + NWS, len(wplan)):
            _issue(wstate["issued"])
            wstate["issued"] += 1
        return wstate["views"].pop(k)

    class Pool_:
        def __init__(self, es, name, shape, dt, n):
            self.t = [es.enter_context(sbt(f"{name}{i}", list(shape), dt)) for i in range(n)]
            self.b = [Buf(f"{name}{i}") for i in range(n)]
            self.k = [f"{name}{i}" for i in range(n)]
            self.i = 0

        def next(self):
            j = self.i % len(self.t)
            self.i += 1
            return self.t[j], self.b[j], self.k[j]

    mm = lambda out, lhsT, rhs, st, sp: (lambda e: e.matmul(out, lhsT, rhs, start=st, stop=sp))

    S.op("dve", lambda e: e.memset(onesD[:], 1.0 / D), writes=[bconst])
    S.op("dve", lambda e: e.memset(ones128[:], 1.0 / 128), writes=[bconst])
    S.op("dve", lambda e: e.memset(ones1[:], 1.0), writes=[bconst])
    S.op("dve", lambda e: e.memset(onesf[:], 1.0), writes=[bconst])
    S.op("dve", lambda e: e.memset(epsb[:], EPS), writes=[bconst])
    S.op("dve", lambda e: e.memset(blk64[:], 0.0), writes=[bconst])
    S.op("dve", lambda e: e.memset(blk64[0:64, 0:64], 1.0 / 64), writes=[bconst])
    S.op("dve", lambda e: e.memset(blk64[64:128, 64:128], 1.0 / 64), writes=[bconst])
    S.dma("sp", [(vec[:], vecs_d)], writes=[bvec], key="vec")
    for c in range(KC):
        S.dma("sp", [(xres[:, c, :], xT_d[:, c, :])], writes=bx[c], key=f"xl{c}")

    GB = 2 * VL
    with ExitStack() as es:
        wad = es.enter_context(sbt("wad", [128, depth, 12, 8, 128], F32)); bwad = [Buf(f"wad{l}") for l in range(depth)]
        sc_t = es.enter_context(sbt("sc_t", [128, 8, 2], F32)); bsc = Buf("sc")
        tmp16 = es.enter_context(sbt("tmp16", [128, 16], F32)); btmp = Buf("tmp16")
        mown = es.enter_context(sbt("mown", [128, MW], F32)); bmown = Buf("mown")
        mgat = es.enter_context(sbt("mgat", [128, 4, MW], F32)); bmgat = Buf("mgat")
        bownM, bgatM = Buf("ownM"), Buf("gatM")
        S.op("act", lambda e: e.activation(out=sc_t[:].rearrange("p a b -> p (a b)"), in_=vec[:, GB:GB + 16],
                                           func=AF.Silu), reads=[bvec], writes=[bsc])
        for l in range(depth):
            S.dma("sp", [(wad[:, l, k], W[l]["wada"][k]) for k in range(12)], writes=[bwad[l]], key=f"wad{l}")
            for k in range(12):
                col = (l * 12 + k) * 2
                S.group([mm(ps[:, 0, col:col + 2], wad[:, l, k, kc, :], sc_t[:, kc, :], kc == 0, kc == KC - 1)
                         for kc in range(KC)], reads=[bwad[l], bsc], writes=[bps[0]])
        S.op("dve", lambda e: e.tensor_copy(mown[:], ps[:, 0, 0:MW]), reads=[bps[0]], writes=[bmown])
        S.dma("sp", [(ownM_d, mown[:])], reads=[bmown], writes=[bownM], key="ownM")
        S.collective([ownM_d.opt()], [gatM_d.opt()], [[0, 1, 2, 3], [4, 5, 6, 7]], reads=[bownM], writes=[bgatM])
        S.dma("sp", [(mgat[:], gatM_d.rearrange("(r p) x -> p r x", p=128))], reads=[bgatM], writes=[bmgat], key="mgat")
        mgv = mgat[:].rearrange("p r (l k c) -> p r l k c", l=depth, k=12)
        for l in range(depth):
            lam_init = 0.8 - 0.6 * math.exp(-0.3 * l)
            vb = l * VL
            for j in range(2):
                S.op("dve", lambda e, j=j: e.tensor_tensor(mods[:, l, :, j].rearrange("p (r k) -> p r k", k=12),
                                                           mgv[:, :, l, :, j],
                                                           vec[:, vb:vb + 48].rearrange("p (r k) -> p r k", k=12), ALU.add),
                     reads=[bmgat, bvec], writes=[bmods])
            for k2 in range(2):
                scv = 1 + 3 * k2
                S.op("dve", lambda e: e.tensor_scalar_add(
                    tmp16[:], mods[:, l, scv * 8:(scv + 1) * 8, :].rearrange("p a b -> p (a b)"), 1.0),
                    reads=[bmods], writes=[btmp])
                for j in range(2):
                    S.op("dve", lambda e, j=j: e.tensor_tensor(
                        a12[:, l, k2, :, j], tmp16[:].rearrange("p (a b) -> p a b", b=2)[:, :, j],
                        vec[:, vb + 48 + 8 * k2: vb + 56 + 8 * k2], ALU.mult),
                        reads=[btmp, bvec], writes=[bmods])
            lc = vb + 94
            S.op("dve", lambda e: e.tensor_tensor(tmp16[0:64, 0:1], vec[0:64, lc:lc + 1], vec[0:64, lc + 1:lc + 2],
                                                  ALU.mult), reads=[bvec, bmods], writes=[btmp])
            S.op("dve", lambda e: e.tensor_tensor(tmp16[0:64, 1:2], vec[0:64, lc + 2:lc + 3], vec[0:64, lc + 3:lc + 4],
                                                  ALU.mult), reads=[bvec], writes=[btmp])
            S.group([mm(ps[:, 2 + l, 0:2], onesf[0:64, :], tmp16[0:64, 0:2], True, True)],
                    reads=[btmp, bconst], writes=[bps[2 + l]])
            S.op("act", lambda e: e.activation(out=tmp16[:, 2:4], in_=ps[:, 2 + l, 0:2], func=AF.Exp),
                 reads=[bps[2 + l]], writes=[btmp])
            S.op("dve", lambda e: e.tensor_tensor(tmp16[:, 4:5], tmp16[:, 3:4], tmp16[:, 2:3], ALU.subtract),
                 reads=[btmp], writes=[btmp])
            S.op("dve", lambda e: e.tensor_scalar_add(lamv[:, l:l + 1], tmp16[:, 4:5], -lam_init),
                 reads=[btmp], writes=[bmods])
            S.op("dve", lambda e: e.tensor_scalar_mul(sgl[:, l:l + 1], vec[:, vb + 88:vb + 89], 1.0 - lam_init),
                 reads=[bvec], writes=[bmods])
        S.barrier()

    def MOD(l, v, c, j):
        return mods[:, l, v * 8 + c, j:j + 1]

    def norm_mod(es_pools, l, k2, dst, bdst, tl):
        sqx, rsP, tmpP = es_pools
        tl = list(tl)
        rts = {}

        def rstd_ops(ti):
            t0, n, j = tiles[ti]
            bk, rt, rb = rts[ti]
            S.op("act", lambda e: e.activation(out=rt[:, 0:n], in_=ps[:, bk, 0:n], func=AF.Ln, bias=epsb[:], scale=1.0),
                 reads=[bps[bk], bconst], writes=[rb])
            S.op("act", lambda e: e.activation(out=rt[:, 0:n], in_=rt[:, 0:n], func=AF.Exp, scale=-0.5),
                 reads=[rb], writes=[rb])

        prev_ti = None
        for ti in tl:
            t0, n, j = tiles[ti]
            st, sbf, _ = sqx.next()
            S.op("act", lambda e: e.activation(out=st[:, :, 0:n], in_=xres[:, :, t0:t0 + n], func=AF.Square),
                 reads=[bx[c][ti] for c in range(KC)], writes=[sbf])
            bk = nextbank()
            S.group([mm(ps[:, bk, 0:n], onesD[:], st[:, c, 0:n], c == 0, c == KC - 1) for c in range(KC)],
                    reads=[sbf, bconst], writes=[bps[bk]])
            rt, rb, _ = rsP.next()
            rts[ti] = (bk, rt, rb)
            if prev_ti is not None:
                rstd_ops(prev_ti)
            prev_ti = ti
        if prev_ti is not None:
            rstd_ops(prev_ti)
        for ti in tl:
            t0, n, j = tiles[ti]
            bk, rt, rb = rts[ti]
            for c in range(KC):
                tt, tb, _ = tmpP.next()
                S.op("dve", lambda e: e.tensor_tensor(tt[:, 0:n], xres[:, c, t0:t0 + n], rt[:, 0:n], ALU.mult),
                     reads=[bx[c][ti], rb], writes=[tb])
                S.op("act", lambda e: e.activation(out=dst[:, c, t0:t0 + n], in_=tt[:, 0:n], func=AF.Identity,
                                                   bias=MOD(l, 3 * k2, c, j), scale=a12[:, l, k2, c, j:j + 1]),
                     reads=[tb, bmods], writes=[bdst[c][ti]])

    groups = [[0, 1, 2, 3], [4, 5, 6, 7]]

    STOP = os.environ.get('KSTOP', '')
    for l in range(depth):
        last = l == depth - 1
        if STOP == 'p0':
            break
        vb = l * VL
        ntl = len(tiles) - 1 if last else len(tiles)
        ownVv = [ownV_d[l][i].rearrange("(h a) (b d) -> h (a b) d", h=2, d=128) for i in range(4)]
        ownH = ownH_d[l].rearrange("r c -> (r c)").rearrange("(s m p x) -> s m p x", s=2, m=8, p=128)

        with ExitStack() as es:
            hT = es.enter_context(sbt("hT", [128, KC, T], BF16))
            bh = [[Buf(f"h{c}_{i}") for i in range(len(tiles))] for c in range(KC)]
            tab = es.enter_context(sbt("tab", [128, 2, T], F32)); btab = Buf("tab")
            S.dma("sp", [(tab[:], tabs_d)], writes=[btab], key="tab")
            with ExitStack() as es2:
                sqx = Pool_(es2, "sqx", [128, KC, 512], BF16, 2)
                rsP = Pool_(es2, "rsP", [128, 512], F32, len(tiles))
                tmpP = Pool_(es2, "tmpP", [128, 512], F32, 3)
                norm_mod((sqx, rsP, tmpP), l, 0, hT, bh, range(len(tiles)))
                S.barrier()
            alloc_wsl(es)
            rsP = Pool_(es, "rsQ", [128, 512], F32, 2)
            sq1 = Pool_(es, "sq1", [128, 512], BF16, 2)
            t1P = Pool_(es, "t1P", [128, 512], F32, 2)
            t2P = Pool_(es, "t2P", [128, 512], F32, 2)
            qoP = Pool_(es, "qoP", [128, 512], BF16, 3)
            vsP = Pool_(es, "vsP", [128, 512], BF16, 3)
            f1P = Pool_(es, "f1P", [128, 512], F32, 3)
            f2P = Pool_(es, "f2P", [128, 512], F32, 3)
            sgP = Pool_(es, "sgP", [128, 2, 512], F32, 2)
            hsP = Pool_(es, "hsP", [128, 16], BF16, 4)
            print("phaseA sbuf remaining", nc.sbuf_bytes_remaining)

            def proj(wv_, wb_, col0, ti, bk):
                t0, n, j = tiles[ti]
                S.group([mm(ps[:, bk, 0:n], wv_[:, kc, col0:col0 + 128], hT[:, kc, t0:t0 + n], kc == 0, kc == KC - 1)
                         for kc in range(KC)], reads=[wb_] + [bh[kc][ti] for kc in range(KC)], writes=[bps[bk]])

            def qk_item(idx, is_k, h):
                wv_, wb_ = load_w(W[l]["wq"][idx], 8, 256)
                gcol = vb + (89 if is_k else 91)
                pend = None
                for ti in range(len(tiles)):
                    t0, n, j = tiles[ti]
                    if (not is_k) and last and j == 1:
                        continue
                    if os.environ.get('KNOCTX') and j == 1:
                        continue
                    if os.environ.get('KNOLAT') and j == 0:
                        continue
                    KQ = int(os.environ.get('KQ', '9'))
                    if KQ < 1:
                        continue
                    b0, b1 = nextbank(), nextbank()
                    proj(wv_, wb_, 0, ti, b0)
                    proj(wv_, wb_, 128, ti, b1)
                    if KQ < 2:
                        continue
                    s1, s1b, _ = sq1.next()
                    if not os.environ.get('KNOSQ'):
                        S.op("act", lambda e: e.activation(out=s1[:, 0:n], in_=ps[:, b0, 0:n], func=AF.Square),
                             reads=[bps[b0]], writes=[s1b])
                    t1, t1b, _ = t1P.next()
                    t2, t2b, _ = t2P.next()
                    S.op("dve", lambda e: e.scalar_tensor_tensor(t1[:, 0:n], ps[:, b0, 0:n], vec[:, gcol:gcol + 1],
                                                                 tab[:, 0, t0:t0 + n], ALU.mult, ALU.mult),
                         reads=[bps[b0], btab, bvec], writes=[t1b])
                    S.op("dve", lambda e: e.scalar_tensor_tensor(t2[:, 0:n], ps[:, b1, 0:n], vec[:, gcol + 1:gcol + 2],
                                                                 tab[:, 1, t0:t0 + n], ALU.mult, ALU.mult),
                         reads=[bps[b1], btab, bvec], writes=[t2b])

                    def fin(ti=ti, t0=t0, n=n, j=j, s1=s1, s1b=s1b, t1=t1, t1b=t1b, t2=t2, t2b=t2b):
                        bn = nextbank()
                        S.group([mm(ps[:, bn, 0:n], blk64[:], s1[:, 0:n], True, True)], reads=[s1b, bconst],
                                writes=[bps[bn]])
                        rt, rb, _ = rsP.next()
                        S.op("act", lambda e: e.activation(out=rt[:, 0:n], in_=ps[:, bn, 0:n], func=AF.Ln,
                                                           bias=epsb[:], scale=1.0), reads=[bps[bn], bconst], writes=[rb])
                        S.op("act", lambda e: e.activation(out=rt[:, 0:n], in_=rt[:, 0:n], func=AF.Exp, scale=-0.5),
                             reads=[rb], writes=[rb])
                        S.op("dve", lambda e: e.tensor_tensor(t1[:, 0:n], t1[:, 0:n], t2[:, 0:n], ALU.add),
                             reads=[t1b, t2b], writes=[t1b])
                        if is_k and j == 1:
                            S.op("dve", lambda e: e.tensor_tensor(ctxK[:, h, :], t1[:, 0:n], rt[:, 0:n], ALU.mult),
                                 reads=[t1b, rb], writes=[bctxK])
                        else:
                            qo, qob, qk_ = qoP.next()
                            S.op("dve", lambda e: e.tensor_tensor(qo[:, 0:n], t1[:, 0:n], rt[:, 0:n], ALU.mult),
                                 reads=[t1b, rb], writes=[qob])
                            if is_k:
                                S.dma("sp", [(ownK_d[l][h // 2][(h % 2) * 128:(h % 2 + 1) * 128, t0:t0 + n], qo[:, 0:n])],
                                      reads=[qob], writes=[bownK[l][h // 2]], key=qk_)
                            else:
                                S.dma("sp", [(QS[h][:, t0:t0 + n], qo[:, 0:n])], reads=[qob], writes=[bQS], key=qk_)
                    if KQ < 3:
                        continue
                    if pend:
                        pend()
                    pend = fin
                if pend:
                    pend()

            def v_item(g):
                wv_, wb_ = load_w(W[l]["wv"][g], 8, 512)
                for ti in range(len(tiles)):
                    t0, n, j = tiles[ti]
                    for s_ in range(n // 128):
                        bk = nextbank()
                        c0 = t0 + s_ * 128
                        S.group([mm(ps[:, bk, :], hT[:, kc, c0:c0 + 128], wv_[:, kc, :], kc == 0, kc == KC - 1)
                                 for kc in range(KC)], reads=[wb_] + [bh[kc][ti] for kc in range(KC)],
                                writes=[bps[bk]])
                        if j == 1:
                            S.op("act", lambda e: e.copy(ctxV[:, s_, g * 512:(g + 1) * 512], ps[:, bk, :]),
                                 reads=[bps[bk]], writes=[bctxV])
                        else:
                            vs, vsb, vk_ = vsP.next()
                            S.op("act", lambda e: e.copy(vs[:], ps[:, bk, :]), reads=[bps[bk]], writes=[vsb])
                            S.dma("sp", [(ownVv[2 * g + k2][:, c0:c0 + 128, :].rearrange("h t d -> t h d"),
                                          vs[:, 256 * k2:256 * (k2 + 1)].rearrange("p (h d) -> p h d", h=2))
                                         for k2 in range(2)], reads=[vsb],
                                  writes=[bownV[l][2 * g], bownV[l][2 * g + 1]], key=vk_)

            def conv_item(m):
                wv_, wb_ = load_w(W[l]["wconv"][m], 8, 384)
                for ti in range(ntl):
                    t0, n, j = tiles[ti]
                    b0, b1, b2 = nextbank(), nextbank(), nextbank()
                    proj(wv_, wb_, 0, ti, b0)
                    proj(wv_, wb_, 128, ti, b1)
                    proj(wv_, wb_, 256, ti, b2)
                    f1, f1b, f1k = f1P.next()
                    f2, f2b, f2k = f2P.next()
                    S.op("act", lambda e: e.copy(f1[:, 0:n], ps[:, b0, 0:n]), reads=[bps[b0]], writes=[f1b])
                    S.op("dve", lambda e: e.tensor_tensor(f1[:, 0:n], ps[:, b1, 0:n], f1[:, 0:n], ALU.mult),
                         reads=[bps[b1], f1b], writes=[f1b])
                    S.dma("sp", [(US[m][:, t0:t0 + n], f1[:, 0:n])], reads=[f1b], writes=[bUS], key=f1k)
                    S.op("act", lambda e: e.copy(f2[:, 0:n], ps[:, b2, 0:n]), reads=[bps[b2]], writes=[f2b])
                    S.dma("sp", [(CBS[m][:, t0:t0 + n], f2[:, 0:n])], reads=[f2b], writes=[bCBS], key=f2k)
                    if j == 0 and t0 == 0:
                        hs, hsb, hk_ = hsP.next()
                        S.op("dve", lambda e: e.tensor_copy(hs[:], f1[:, 0:16]), reads=[f1b], writes=[hsb])
                        S.dma("sp", [(ownH[0, m], hs[:])], reads=[hsb], writes=[bownH[l]], key=hk_)
                    if j == 0 and t0 + n == NLAT:
                        hs, hsb, hk_ = hsP.next()
                        S.op("dve", lambda e: e.tensor_copy(hs[:], f1[:, n - 16:n]), reads=[f1b], writes=[hsb])
                        S.dma("sp", [(ownH[1, m], hs[:])], reads=[hsb], writes=[bownH[l]], key=hk_)

            def gate_item(m):
                wv_, wb_ = load_w(W[l]["wgt"][m], 8, 256)
                for ti in range(ntl):
                    t0, n, j = tiles[ti]
                    b0, b1 = nextbank(), nextbank()
                    proj(wv_, wb_, 0, ti, b0)
                    proj(wv_, wb_, 128, ti, b1)
                    sg, sgb, sgk = sgP.next()
                    S.op("act", lambda e: e.activation(out=sg[:, 0, 0:n], in_=ps[:, b0, 0:n], func=AF.Sigmoid),
                         reads=[bps[b0]], writes=[sgb])
                    S.op("act", lambda e: e.activation(out=sg[:, 1, 0:n], in_=ps[:, b1, 0:n], func=AF.Sigmoid),
                         reads=[bps[b1]], writes=[sgb])
                    S.dma("sp", [(SGS[m][:, :, t0:t0 + n], sg[:, :, 0:n])], reads=[sgb], writes=[bSGS], key=sgk)

            set_plan([(W[l]["wq"][i], 8, 256) for i in range(8)] + [(W[l]["wv"][g], 8, 512) for g in range(2)]
                     + [(W[l]["wconv"][m], 8, 384) for m in range(8)] + [(W[l]["wq"][8 + i], 8, 256) for i in range(8)]
                     + [(W[l]["wgt"][m], 8, 256) for m in range(8)])
            for h in range(int(os.environ.get('KITEMS', H))):
                if STOP != 'A0':
                    qk_item(h, True, h)
                    if h % 2 == 1 and STOP not in ('A1', 'A2', 'A3'):
                        S.collective([ownK_d[l][h // 2].opt()], [gatK_d[l][h // 2].opt()], groups,
                                     reads=[bownK[l][h // 2]], writes=[bgatK[l][h // 2]])
            for g in range(2):
                if STOP not in ('A0', 'A1'):
                    v_item(g)
                    if STOP not in ('A2', 'A3'):
                        for i in (2 * g, 2 * g + 1):
                            S.collective([ownV_d[l][i].opt()], [gatV_d[l][i].opt()], groups,
                                         reads=[bownV[l][i]], writes=[bgatV[l][i]])
            for m in range(8):
                if STOP not in ('A0', 'A1', 'A2'):
                    conv_item(m)
            if STOP not in ('A0', 'A1', 'A2', 'A3'):
                S.collective([ownH_d[l].opt()], [gatH_d[l].opt()], groups, reads=[bownH[l]], writes=[bgatH[l]])
            for h in range(H):
                if STOP not in ('A0', 'A1', 'A2', 'A3', 'A4'):
                    qk_item(8 + h, False, h)
            for m in range(8):
                if STOP not in ('A0', 'A1', 'A2', 'A3', 'A4', 'A5'):
                    gate_item(m)
            S.barrier()

        if STOP.startswith('A'):
            break
        with ExitStack() as es:
            khP = Pool_(es, "kh", [128, NKL], BF16, 2)
            vhP = Pool_(es, "vh", [128, NKL // 128, 128], BF16, 2)
            qhP = Pool_(es, "qh", [128, T], BF16, 2)
            NPB = 3
            pbT = [es.enter_context(sbt(f"pb{i}", [128, 2, 2, 512], BF16)) for i in range(NPB)]
            pbB = [[Buf(f"pb{i}_{k}") for k in range(2)] for i in range(NPB)]
            pbctr = [0]
            rzP = Pool_(es, "rz", [128, 2, 512], F32, 1)
            o1P = Pool_(es, "o1", [128, 512], F32, 1)
            o2P = Pool_(es, "o2", [128, 512], F32, 1)
            osP = Pool_(es, "osq", [128, 512], BF16, 1)
            rnP = Pool_(es, "rn", [128, 512], F32, 1)
            atP = Pool_(es, "at", [128, 512], BF16, 2)
            paccP = Pool_(es, "pacc", [128, 3, 512], F32, 2)
            zoP = Pool_(es, "zo", [128, 512], F32, 1)
            print("phaseB1 sbuf remaining", nc.sbuf_bytes_remaining)

            def load_head(h):
                kt, kb, kk_ = khP.next()
                vt, vb_, vk_ = vhP.next()
                qt, qb, qk_ = qhP.next()
                S.dma("sp", [(qt[:], QS[h])], reads=[bQS], writes=[qb], key=qk_)
                hp, ho = h // 2, (h % 2) * 128
                S.dma("sp", [(kt[:, r * NLAT:(r + 1) * NLAT], gatK_d[l][hp][r * 256 + ho: r * 256 + ho + 128, :])
                             for r in range(4)], reads=[bgatK[l][hp]], writes=[kb], key=kk_)
                pairs = []
                for r in range(4):
                    src = gatV_d[l][hp][r * 256 + ho: r * 256 + ho + 128, :]
                    src = src.rearrange("a (b d) -> (a b) d", d=128).rearrange("(c p) d -> p c d", p=128)
                    pairs.append((vt[:, r * B:(r + 1) * B, :], src))
                S.dma("sp", pairs, reads=[bgatV[l][hp]], writes=[vb_], key=vk_)
                return (kt, kb, vt, vb_, qt, qb)

            pending = []
            nxt = load_head(0)
            for h in range(H):
                kt, kb, vt, vb_, qt, qb = nxt
                if h + 1 < H:
                    nxt = load_head(h + 1)
                for ti in range(ntl):
                    t0, n, j = tiles[ti]
                    nk = 2 if j == 1 else 2 + NKL // 128

                    def kv(c):
                        if c < 2:
                            return ctxK[:, h, c * 128:(c + 1) * 128], ctxV[:, c, h * 128:(h + 1) * 128], [bctxK, bctxV]
                        return kt[:, (c - 2) * 128:(c - 1) * 128], vt[:, c - 2, :], [kb, vb_]

                    def s_mm(c):
                        kk, _, bb = kv(c)
                        s0 = 2 * (c % 2)
                        S.group([mm(ps[:, s0, 0:n], kk[0:64, :], qt[0:64, t0:t0 + n], True, True),
                                 mm(ps[:, s0 + 1, 0:n], kk[64:128, :], qt[64:128, t0:t0 + n], True, True)],
                                reads=bb + [qb], writes=[bps[s0], bps[s0 + 1]])

                    def e_op(c):
                        s0 = 2 * (c % 2)
                        if c % 2 == 0:
                            pbctr[0] += 1
                        i = pbctr[0] % NPB
                        pt, pbb = pbT[i], pbB[i][c % 2]
                        S.op("act", lambda e: e.activation(out=pt[:, c % 2, :, 0:n], in_=ps[:, s0:s0 + 2, 0:n], func=AF.Exp,
                                                           scale=ATTN_SCALE), reads=[bps[s0], bps[s0 + 1]], writes=[pbb])
                        return pt, pbb, i

                    def pv_mm(c, pt, pbb, i):
                        _, vv, bb = kv(c)
                        st_, sp_ = c == 0, c == nk - 1
                        fns = [mm(ps[:, 4, 0:n], vv, pt[:, c % 2, 0, 0:n], st_, sp_),
                               mm(ps[:, 5, 0:n], vv, pt[:, c % 2, 1, 0:n], st_, sp_)]
                        wr = [bps[4], bps[5]]
                        if c % 2 == 1:
                            fns.append(mm(ps[:, 7, 0:n], ones1[:], pt[:, 1, 1, 0:n], c == 1, c == nk - 1))
                            wr.append(bps[7])
                        S.group(fns, reads=bb + [pbb, bconst], writes=wr)

                    pacc, paccb, _ = paccP.next()

                    def z_acc(c, pt, pbb, i):
                        if c % 2 == 0:
                            return
                        p3 = pt[:].rearrange("p a b q -> p (a b) q")[:, 0:3, 0:n]
                        if c == 1:
                            S.op("dve", lambda e: e.tensor_copy(pacc[:, :, 0:n], p3), reads=pbB[i], writes=[paccb])
                        else:
                            S.op("dve", lambda e: e.tensor_tensor(pacc[:, :, 0:n], pacc[:, :, 0:n], p3, ALU.add),
                                 reads=pbB[i] + [paccb], writes=[paccb])

                    s_mm(0)
                    for c in range(nk):
                        cur = e_op(c)
                        z_acc(c, *cur)
                        if c + 1 < nk:
                            s_mm(c + 1)
                        pv_mm(c, *cur)
                        while pending and pending[0][0] <= c:
                            pending.pop(0)[1]()
                    while pending:
                        pending.pop(0)[1]()
                    o1, o1b, _ = o1P.next(); o2, o2b, _ = o2P.next()
                    rz, rzb, _ = rzP.next()
                    rn, rnb, _ = rnP.next()
                    osq, osb, _ = osP.next()
                    zo, zob, _ = zoP.next()
                    S.op("dve", lambda e: e.tensor_copy(o1[:, 0:n], ps[:, 4, 0:n]), reads=[bps[4]], writes=[o1b])
                    S.op("dve", lambda e: e.tensor_copy(o2[:, 0:n], ps[:, 5, 0:n]), reads=[bps[5]], writes=[o2b])
                    S.op("dve", lambda e: e.tensor_copy(zo[:, 0:n], ps[:, 7, 0:n]), reads=[bps[7]], writes=[zob])

                    def e2a(n=n, pacc=pacc, paccb=paccb):
                        S.group([mm(ps[:, 6, 0:n], onesf[:], pacc[:, 0, 0:n], True, False),
                                 mm(ps[:, 6, 0:n], onesf[:], pacc[:, 2, 0:n], False, True)],
                                reads=[paccb, bconst], writes=[bps[6]])

                    def e3a(n=n, rz=rz, rzb=rzb):
                        S.op("act", lambda e: e.activation(out=rz[:, 0, 0:n], in_=ps[:, 6, 0:n], func=AF.Ln),
                             reads=[bps[6]], writes=[rzb])

                    def e2b(n=n, pacc=pacc, paccb=paccb):
                        S.group([mm(ps[:, 6, 0:n], onesf[:], pacc[:, 1, 0:n], True, True)],
                                reads=[paccb, bconst], writes=[bps[6]])

                    def e34(n=n, o1=o1, o1b=o1b, o2=o2, o2b=o2b, rz=rz, rzb=rzb, osq=osq, osb=osb, zo=zo, zob=zob):
                        S.op("dve", lambda e: e.tensor_tensor(rz[:, 1, 0:n], ps[:, 6, 0:n], zo[:, 0:n], ALU.add),
                             reads=[bps[6], zob, rzb], writes=[rzb])
                        S.op("act", lambda e: e.activation(out=rz[:, 1, 0:n], in_=rz[:, 1, 0:n], func=AF.Ln),
                             reads=[rzb], writes=[rzb])
                        S.op("act", lambda e: e.activation(out=rz[:, :, 0:n], in_=rz[:, :, 0:n], func=AF.Exp, scale=-1.0),
                             reads=[rzb], writes=[rzb])
                        S.op("dve", lambda e: e.tensor_tensor(o1[:, 0:n], o1[:, 0:n], rz[:, 0, 0:n], ALU.mult),
                             reads=[o1b, rzb], writes=[o1b])
                        S.op("dve", lambda e: e.tensor_tensor(o2[:, 0:n], o2[:, 0:n], rz[:, 1, 0:n], ALU.mult),
                             reads=[o2b, rzb], writes=[o2b])
                        S.op("dve", lambda e: e.scalar_tensor_tensor(o1[:, 0:n], o2[:, 0:n], lamv[:, l:l + 1], o1[:, 0:n],
                                                                     ALU.mult, ALU.add), reads=[o2b, o1b, bmods], writes=[o1b])
                        S.op("dve", lambda e: e.tensor_tensor(osq[:, 0:n], o1[:, 0:n], o1[:, 0:n], ALU.mult),
                             reads=[o1b], writes=[osb])

                    def e5(n=n, osq=osq, osb=osb):
                        S.group([mm(ps[:, 6, 0:n], ones128[:], osq[:, 0:n], True, True)], reads=[osb, bconst],
                                writes=[bps[6]])

                    def e67(n=n, t0=t0, h=h, o1=o1, o1b=o1b, rn=rn, rnb=rnb):
                        S.op("act", lambda e: e.activation(out=rn[:, 0:n], in_=ps[:, 6, 0:n], func=AF.Ln, bias=epsb[:],
                                                           scale=1.0), reads=[bps[6], bconst], writes=[rnb])
                        S.op("act", lambda e: e.activation(out=rn[:, 0:n], in_=rn[:, 0:n], func=AF.Exp, scale=-0.5),
                             reads=[rnb], writes=[rnb])
                        at, atb, atk = atP.next()
                        S.op("dve", lambda e: e.scalar_tensor_tensor(at[:, 0:n], o1[:, 0:n], sgl[:, l:l + 1], rn[:, 0:n],
                                                                     ALU.mult, ALU.mult), reads=[o1b, rnb, bmods], writes=[atb])
                        S.dma("sp", [(ATS[h][:, t0:t0 + n], at[:, 0:n])], reads=[atb], writes=[bATS], key=atk)

                    pending.extend([(1, e2a), (2, e3a), (3, e2b), (4, e34), (6, e5), (8, e67)])
            while pending:
                pending.pop(0)[1]()
            S.barrier()

        if STOP == 'B1':
            break
        with ExitStack() as es:
            wres = [es.enter_context(sbt(f"wres{i}", [128, 8, 1024], BF16)) for i in range(3)]
            bwres = [Buf(f"wres{i}") for i in range(3)]
            attP = Pool_(es, "att", [128, 8, 512], BF16, 2)
            ycP = Pool_(es, "yc", [128, 8, 512], BF16, 2)
            mgP = Pool_(es, "mg", [128, 8, 512], BF16, 1)
            ueP = Pool_(es, "ue", [128, 514], F32, 2)
            cbP = Pool_(es, "cbt", [128, 512], F32, 2)
            c1P = Pool_(es, "c1", [128, 512], F32, 2)
            sgP2 = Pool_(es, "sg2", [128, 2, 512], F32, 2)
            m1P = Pool_(es, "m1", [128, 512], F32, 2)
            m2P = Pool_(es, "m2", [128, 512], F32, 2)
            hl = es.enter_context(sbt("hl", [128, 4, 2, 8, 16], BF16)); bhl = Buf("hl")
            print("phaseB2 sbuf remaining", nc.sbuf_bytes_remaining)
            bwres = [[Buf(f"wres{i}_{m}") for m in range(8)] for i in range(3)]
            for i in (0, 1, 2):
                for m in range(8):
                    S.dma("pool", [(wres[i][:, :, m * 128:(m + 1) * 128], W[l]["wp"][i][:, :, m * 128:(m + 1) * 128])],
                          writes=[bwres[i][m]], key=f"wres{i}_{m}")
            pairs = []
            for r in range(4):
                src = gatH_d[l][r * HR: (r + 1) * HR, :].rearrange("r c -> (r c)").rearrange(
                    "(s m p x) -> p s m x", s=2, m=8, p=128)
                for s_ in range(2):
                    pairs.append((hl[:, r, s_], src[:, s_]))
            S.dma("sp", pairs, reads=[bgatH[l]], writes=[bhl], key="hl")
            for s_ in range(2):
                mc = GB + 24 + 4 * s_
                for r in range(4):
                    src = hl[:, r, 1 - s_, :, 15 if s_ == 0 else 0]
                    if r == 0:
                        S.op("dve", lambda e: e.tensor_scalar_mul(halo[:, s_, :], src, vec[:, mc:mc + 1]),
                             reads=[bhl, bvec], writes=[bhalo])
                    else:
                        S.op("dve", lambda e: e.scalar_tensor_tensor(halo[:, s_, :], src, vec[:, mc + r:mc + r + 1],
                                                                     halo[:, s_, :], ALU.mult, ALU.add),
                             reads=[bhl, bvec, bhalo], writes=[bhalo])
            def b2_prep(ti):
                t0, n, j = tiles[ti]
                att, attb, attk = attP.next()
                S.dma("sp", [(att[:, :, 0:n], ATS[:, :, t0:t0 + n].rearrange("h p t -> p h t"))], reads=[bATS],
                      writes=[attb], key=attk)
                yc, ycb, _ = ycP.next()
                lo_edge = (j == 1) or t0 == 0
                hi_edge = (j == 1) or (t0 + n == NLAT)
                for m in range(8):
                    ue, ueb, uek = ueP.next()
                    a0 = t0 if lo_edge else t0 - 1
                    a1 = t0 + n if hi_edge else t0 + n + 1
                    S.dma("sp", [(ue[:, 1 + a0 - t0: 1 + a1 - t0], US[m][:, a0:a1])], reads=[bUS], writes=[ueb], key=uek)
                    if lo_edge:
                        if j == 1:
                            S.op("dve", lambda e: e.memset(ue[:, 0:1], 0.0), writes=[ueb])
                        else:
                            S.op("dve", lambda e: e.tensor_copy(ue[:, 0:1], halo[:, 0, m:m + 1]), reads=[bhalo],
                                 writes=[ueb])
                    if hi_edge:
                        if j == 1:
                            S.op("dve", lambda e: e.memset(ue[:, n + 1:n + 2], 0.0), writes=[ueb])
                        else:
                            S.op("dve", lambda e: e.tensor_copy(ue[:, n + 1:n + 2], halo[:, 1, m:m + 1]),
                                 reads=[bhalo], writes=[ueb])
                    cbt, cbb, cbk = cbP.next()
                    S.dma("sp", [(cbt[:, 0:n], CBS[m][:, t0:t0 + n])], reads=[bCBS], writes=[cbb], key=cbk)
                    c1, c1b, _ = c1P.next()
                    cw = vb + 64
                    S.op("dve", lambda e: e.tensor_scalar_mul(c1[:, 0:n], ue[:, 1:n + 1], vec[:, cw + 8 + m:cw + 9 + m]),
                         reads=[ueb, bvec], writes=[c1b])
                    S.op("dve", lambda e: e.scalar_tensor_tensor(c1[:, 0:n], ue[:, 0:n], vec[:, cw + m:cw + m + 1],
                                                                 c1[:, 0:n], ALU.mult, ALU.add),
                         reads=[ueb, bvec, c1b], writes=[c1b])
                    S.op("dve", lambda e: e.scalar_tensor_tensor(c1[:, 0:n], ue[:, 2:n + 2],
                                                                  vec[:, cw + 16 + m:cw + 17 + m], c1[:, 0:n],
                                                                  ALU.mult, ALU.add), reads=[ueb, bvec, c1b], writes=[c1b])
                    S.op("dve", lambda e: e.tensor_tensor(yc[:, m, 0:n], c1[:, 0:n], cbt[:, 0:n], ALU.mult),
                         reads=[c1b, cbb], writes=[ycb])
                return att, attb, yc, ycb

            def b2_compute(ti, att, attb, yc, ycb):
                t0, n, j = tiles[ti]
                mg, mgb, _ = mgP.next()
                for m in range(8):
                    b0, b1 = nextbank(), nextbank()
                    S.group([mm(ps[:, b0, 0:n], wres[0][:, k, m * 128:(m + 1) * 128], att[:, k, 0:n], k == 0, k == 7)
                             for k in range(8)], reads=[bwres[0][m], attb], writes=[bps[b0]])
                    S.group([mm(ps[:, b1, 0:n], wres[1][:, k, m * 128:(m + 1) * 128], yc[:, k, 0:n], k == 0, k == 7)
                             for k in range(8)], reads=[bwres[1][m], ycb], writes=[bps[b1]])
                    sg, sgb, sgk = sgP2.next()
                    S.dma("sp", [(sg[:, :, 0:n], SGS[m][:, :, t0:t0 + n])], reads=[bSGS], writes=[sgb], key=sgk)
                    m1, m1b, _ = m1P.next(); m2, m2b, _ = m2P.next()
                    S.op("dve", lambda e: e.tensor_tensor(m1[:, 0:n], ps[:, b0, 0:n], sg[:, 0, 0:n], ALU.mult),
                         reads=[bps[b0], sgb], writes=[m1b])
                    S.op("dve", lambda e: e.tensor_tensor(m2[:, 0:n], ps[:, b1, 0:n], sg[:, 1, 0:n], ALU.mult),
                         reads=[bps[b1], sgb], writes=[m2b])
                    S.op("dve", lambda e: e.tensor_tensor(mg[:, m, 0:n], m1[:, 0:n], m2[:, 0:n], ALU.add),
                         reads=[m1b, m2b], writes=[mgb])
                for m in range(8):
                    b0 = nextbank()
                    S.group([mm(ps[:, b0, 0:n], wres[2][:, k, m * 128:(m + 1) * 128], mg[:, k, 0:n], k == 0, k == 7)
                             for k in range(8)], reads=[bwres[2][m], mgb], writes=[bps[b0]])
                    S.op("dve", lambda e: e.scalar_tensor_tensor(xres[:, m, t0:t0 + n], ps[:, b0, 0:n], MOD(l, 2, m, j),
                                                                 xres[:, m, t0:t0 + n], ALU.mult, ALU.add),
                         reads=[bps[b0], bmods, bx[m][ti]], writes=[bx[m][ti]])
            nxt_p = b2_prep(0)
            for ti in range(ntl):
                cur_p = nxt_p
                if ti + 1 < ntl:
                    nxt_p = b2_prep(ti + 1)
                b2_compute(ti, *cur_p)
            S.barrier()

        if STOP == 'B2':
            break
        with ExitStack() as es:
            sbs, curb, wsum = [], [], 0
            for ti in range(ntl):
                if wsum + tiles[ti][1] > 1280:
                    sbs.append(curb); curb, wsum = [], 0
                curb.append(ti); wsum += tiles[ti][1]
            sbs.append(curb)
            hf = es.enter_context(sbt("hf", [128, KC, T], BF16))
            bhf = [[Buf(f"hf{c}_{i}") for i in range(len(tiles))] for c in range(KC)]
            act = es.enter_context(sbt("actb", [128, FC, 1280], BF16))
            bact = [[Buf(f"act{f}_{i}") for i in range(len(tiles))] for f in range(FC)]
            with ExitStack() as es2:
                sqx = Pool_(es2, "sqx", [128, KC, 512], BF16, 2)
                rsP = Pool_(es2, "rsP", [128, 512], F32, len(tiles))
                tmpP = Pool_(es2, "tmpP", [128, 512], F32, 3)
                norm_mod((sqx, rsP, tmpP), l, 1, hf, bhf, range(ntl))
                S.barrier()
            alloc_wsl(es)
            siP = Pool_(es, "si", [128, 512], F32, 2)
            print("phaseB3 sbuf remaining", nc.sbuf_bytes_remaining)
            for sblk in sbs:
                offs, o = {}, 0
                for ti in sblk:
                    offs[ti] = o
                    o += tiles[ti][1]
                set_plan([(W[l]["wff"][f], 8, 256) for f in range(FC)] + [(W[l]["wdn"][m], FC, 128) for m in range(8)])
                for f in range(FC):
                    wv_, wb_ = load_w(W[l]["wff"][f], 8, 256)
                    for ti in sblk:
                        t0, n, j = tiles[ti]
                        b0, b1 = nextbank(), nextbank()
                        for bk, c0 in ((b0, 0), (b1, 128)):
                            S.group([mm(ps[:, bk, 0:n], wv_[:, kc, c0:c0 + 128], hf[:, kc, t0:t0 + n], kc == 0, kc == KC - 1)
                                     for kc in range(KC)], reads=[wb_] + [bhf[kc][ti] for kc in range(KC)],
                                    writes=[bps[bk]])
                        si, sib, _ = siP.next()
                        S.op("act", lambda e: e.activation(out=si[:, 0:n], in_=ps[:, b0, 0:n], func=AF.Silu),
                             reads=[bps[b0]], writes=[sib])
                        S.op("dve", lambda e: e.tensor_tensor(act[:, f, offs[ti]:offs[ti] + n], ps[:, b1, 0:n], si[:, 0:n],
                                                              ALU.mult), reads=[bps[b1], sib], writes=[bact[f][ti]])
                for m in range(8):
                    wv_, wb_ = load_w(W[l]["wdn"][m], FC, 128)
                    for ti in sblk:
                        t0, n, j = tiles[ti]
                        b0 = nextbank()
                        S.group([mm(ps[:, b0, 0:n], wv_[:, f, :], act[:, f, offs[ti]:offs[ti] + n], f == 0, f == FC - 1)
                                 for f in range(FC)], reads=[wb_] + [bact[f][ti] for f in range(FC)], writes=[bps[b0]])
                        S.op("dve", lambda e: e.scalar_tensor_tensor(xres[:, m, t0:t0 + n], ps[:, b0, 0:n],
                                                                     MOD(l, 5, m, j), xres[:, m, t0:t0 + n],
                                                                     ALU.mult, ALU.add),
                             reads=[bps[b0], bmods, bx[m][ti]], writes=[bx[m][ti]])
            S.barrier()
        if STOP == 'L0':
            break

    bout = Buf("out")
    S.dma("sp", [(yT_d[:, c, :], xres[:, c, 0:NLAT]) for c in range(KC)],
          reads=[bx[c][ti] for c in range(KC) for ti in range(len(tiles) - 1)], writes=[bout], key="out")
    S._waits("sp", [bout], [])
    return nc


_CACHE = {}


def kernel(x, c, ctx, c_ctx, w_ada, b_ada, norm1_g, norm2_g, w_in, q_norm_g, k_norm_g,
           lambda_q1, lambda_k1, lambda_q2, lambda_k2, subln_g, conv_w, w_pa, w_pc, w_o,
           w_ffn_gate, w_ffn_up, w_ffn_down):
    f = lambda a: np.asarray(a, np.float32)
    x, c, ctx, c_ctx = f(x), f(c), f(ctx), f(c_ctx)
    Bn, SEQ, _ = x.shape
    depth = w_in.shape[0]
    NLAT = SEQ // 4
    T = NLAT + NCTX
    if NLAT not in _CACHE:
        _CACHE[NLAT] = build(NLAT, depth)
    nc = _CACHE[NLAT]
    shared = {}
    for l in range(depth):
        lw = _layer_weights(f(w_in[l]), f(w_pa[l]), f(w_pc[l]), f(w_o[l]), f(w_ffn_gate[l]), f(w_ffn_up[l]),
                            f(w_ffn_down[l]))
        for k, v in lw.items():
            shared[f"{k}{l}"] = v
    p = np.arange(128)
    in_maps = []
    for core in range(8):
        b, j = core // 4, core % 4
        xs = np.concatenate([x[b, j * NLAT:(j + 1) * NLAT], ctx[b]], axis=0)
        xT = np.ascontiguousarray(xs.reshape(T, KC, 128).transpose(2, 1, 0))
        vecs = np.zeros((128, NV), np.float32)
        for l in range(depth):
            vb = l * VL
            vecs[:, vb:vb + 48] = _fm(b_ada[l])
            vecs[:, vb + 48:vb + 56] = _fm(norm1_g[l])
            vecs[:, vb + 56:vb + 64] = _fm(norm2_g[l])
            cw = f(conv_w[l])
            for k in range(3):
                vecs[:, vb + 64 + 8 * k: vb + 72 + 8 * k] = _fm(cw[k])
            vecs[:, vb + 88] = f(subln_g[l])
            gk, gq = f(k_norm_g[l]), f(q_norm_g[l])
            vecs[:, vb + 89] = gk[p % 64]
            vecs[:, vb + 90] = gk[((p % 64) + 32) % 64]
            vecs[:, vb + 91] = gq[p % 64]
            vecs[:, vb + 92] = gq[((p % 64) + 32) % 64]
            for i, lv in enumerate((lambda_q1, lambda_k1, lambda_q2, lambda_k2)):
                vecs[0:64, vb + 94 + i] = f(lv[l])
        GB = 2 * VL
        cT = np.stack([_fm(c[b]), _fm(c_ctx)], axis=-1)
        vecs[:, GB:GB + 16] = cT.reshape(128, 16)
        if j > 0:
            vecs[:, GB + 24 + (j - 1)] = 1.0
        if j < 3:
            vecs[:, GB + 28 + (j + 1)] = 1.0
        m = {"xT": xT, "tabs": _rope_tabs(j * NLAT, NLAT), "vecs": vecs}
        for l in range(depth):
            wa = f(w_ada[l])
            m[f"wada{l}"] = np.ascontiguousarray(np.stack(
                [_wch(wa, 128 * (j * 12 + k) + np.arange(128)) for k in range(12)]))
        m.update(shared)
        in_maps.append(m)
    res = run_bass_kernel_spmd(nc, in_maps, core_ids=list(range(8)))
    out = np.zeros((Bn, SEQ, D), np.float32)
    for core in range(8):
        b, j = core // 4, core % 4
        yT = np.asarray(res.results[core]["yT"], np.float32)
        out[b, j * NLAT:(j + 1) * NLAT] = yT.transpose(2, 1, 0).reshape(NLAT, D)
    return out
```

```python
import math
import os
from contextlib import ExitStack
import numpy as np
import concourse.bass as bass
import concourse.mybir as mybir
from concourse.bass_utils import run_bass_kernel_spmd

F32, BF16 = mybir.dt.float32, mybir.dt.bfloat16
ALU = mybir.AluOpType
AF = mybir.ActivationFunctionType

D = 1024
KC = 8
NCTX = 256
H = 8
DFF = 2816
FC = 22
EPS = 1e-6
GRID_W = 64
OFF_K, OFF_V, OFF_CX, OFF_CB, OFF_CC, OFF_GA, OFF_GC = 1024, 2048, 3072, 4096, 5120, 6144, 7168
VL = 98
NV = 2 * VL + 34
MW = 48
ATTN_SCALE = 0.125


class Buf:
    __slots__ = ("w", "rs", "multi", "name", "excl")

    def __init__(self, name="", multi=False, excl=False):
        self.excl = excl
        self.w = {}
        self.rs = {}
        self.multi = multi
        self.name = name


def _add(d, tok):
    if tok is None:
        return
    k = id(tok[0])
    if k not in d or d[k][1] < tok[1]:
        d[k] = tok


class Sched:
    ENG = ("pe", "act", "dve", "pool", "sp")
    LIMIT = 20000

    def __init__(self, nc):
        self.nc = nc
        self.e = {"pe": nc.tensor, "act": nc.scalar, "dve": nc.vector, "pool": nc.gpsimd, "sp": nc.sync}
        self.cur = {}
        self.nsem = 0
        for en in ("pe", "act", "dve", "pool"):
            self.cur[en] = [self._newsem(en), 0]
        self.seen = {en: {} for en in self.ENG}
        self.dsem = {}
        self.latest = {}

    def _newsem(self, name):
        self.nsem += 1
        return self.nc.alloc_semaphore(f"s_{name}_{self.nsem}")

    def _waits(self, eng, reads, writes):
        need = {}
        for b in reads:
            for t in b.w.values():
                _add(need, t)
            if b.excl:
                for t in b.rs.values():
                    _add(need, t)
        for b in writes:
            if not b.multi:
                for t in b.w.values():
                    _add(need, t)
            for t in b.rs.values():
                _add(need, t)
        self._emit_waits(eng, need.values())

    def _emit_waits(self, eng, toks):
        seen = self.seen[eng]
        for sem, val in toks:
            if eng == "pe" and sem is self.cur["pe"][0]:
                continue
            k = id(sem)
            if seen.get(k, 0) >= val:
                continue
            self.e[eng].wait_ge(sem, val)
            seen[k] = val

    def _record(self, tok, reads, writes):
        self.latest[id(tok[0])] = tok
        for b in reads:
            if b.excl:
                b.w = {id(tok[0]): tok}
                b.rs = {}
            else:
                _add(b.rs, tok)
        for b in writes:
            if b.multi:
                _add(b.w, tok)
            else:
                b.w = {id(tok[0]): tok}
                b.rs = {}

    def op(self, eng, fn, reads=(), writes=()):
        self._waits(eng, reads, writes)
        inst = fn(self.e[eng])
        c = self.cur[eng]
        if c[1] >= self.LIMIT:
            c[0] = self._newsem(eng)
            c[1] = 0
        c[1] += 1
        inst.then_inc(c[0], 1)
        self._record((c[0], c[1]), reads, writes)

    def group(self, fns, reads=(), writes=()):
        self._waits("pe", reads, writes)
        inst = None
        for fn in fns:
            inst = fn(self.e["pe"])
        c = self.cur["pe"]
        if c[1] >= self.LIMIT:
            c[0] = self._newsem("pe")
            c[1] = 0
        c[1] += 1
        inst.then_inc(c[0], 1)
        self._record((c[0], c[1]), reads, writes)

    def dma(self, q, pairs, reads=(), writes=(), key=None):
        self._waits(q, reads, writes)
        if key not in self.dsem:
            self.dsem[key] = [self._newsem("d"), 0]
        c = self.dsem[key]
        if c[1] >= 16 * 1000:
            c[0] = self._newsem("d")
            c[1] = 0
        for out, in_ in pairs:
            self.e[q].dma_start(out=out, in_=in_).then_inc(c[0], 16)
            c[1] += 16
        self._record((c[0], c[1]), reads, writes)

    def collective(self, ins, outs, groups, reads, writes):
        self._waits("pool", reads, writes)
        sem = self._newsem("cc")
        self.e["pool"].collective_compute(
            "AllGather", ALU.bypass, replica_groups=groups, ins=ins, outs=outs
        ).then_inc(sem)
        self._record((sem, 1), reads, writes)

    def barrier(self):
        toks = list(self.latest.values())
        for en in self.ENG:
            self._emit_waits(en, toks)


def _fm(v):
    return np.ascontiguousarray(np.asarray(v, np.float32).reshape(-1, 128).T)


def _wch(w, cols):
    K = w.shape[0]
    wk = w.reshape(K // 128, 128, -1)
    return np.ascontiguousarray(wk[:, :, cols].transpose(1, 0, 2))


def _swap_cols(base):
    i = np.arange(128)
    return base + (i // 64) * 64 + ((i % 64) + 32) % 64


def _layer_weights(w_in, w_pa, w_pc, w_o, wg, wu, wd):
    wq = []
    for off in (OFF_K, 0):
        for h in range(H):
            base = off + 128 * h
            cols = np.concatenate([base + np.arange(128), _swap_cols(base)])
            wq.append(_wch(w_in, cols))
    wv = [_wch(w_in, OFF_V + 512 * g + np.arange(512)) for g in range(2)]
    wconv = [_wch(w_in, np.concatenate([o + 128 * m + np.arange(128) for o in (OFF_CX, OFF_CC, OFF_CB)]))
             for m in range(8)]
    wgt = [_wch(w_in, np.concatenate([o + 128 * m + np.arange(128) for o in (OFF_GA, OFF_GC)]))
           for m in range(8)]
    wp = [_wch(w, np.arange(1024)) for w in (w_pa, w_pc, w_o)]
    wff = [np.concatenate([_wch(wg, 128 * f + np.arange(128)), _wch(wu, 128 * f + np.arange(128))], axis=2)
           for f in range(FC)]
    wdn = [_wch(wd, 128 * m + np.arange(128)) for m in range(8)]
    st = lambda xs: np.ascontiguousarray(np.stack(xs))
    return dict(wq=st(wq), wv=st(wv), wconv=st(wconv), wgt=st(wgt), wp=st(wp), wff=st(wff),
                wdn=st(wdn))


def _rope_tabs(pos0, nlat):
    t = np.arange(pos0, pos0 + nlat)
    row = (t // GRID_W).astype(np.float32)
    col = (t % GRID_W).astype(np.float32)
    inv = (np.float32(10000.0) ** (-np.arange(16, dtype=np.float32) / np.float32(16))).astype(np.float32)
    ang = np.concatenate([row[:, None] * inv, col[:, None] * inv], axis=-1).astype(np.float32)
    cos, sin = np.cos(ang).astype(np.float32), np.sin(ang).astype(np.float32)
    T = nlat + NCTX
    tabs = np.zeros((128, 2, T), np.float32)
    tabs[:, 0, nlat:] = 1.0
    p = np.arange(128)
    tabs[:, 0, :nlat] = cos.T[p % 32]
    sgn = np.where((p % 64) < 32, -1.0, 1.0).astype(np.float32)
    tabs[:, 1, :nlat] = sin.T[p % 32] * sgn[:, None]
    return tabs


def build(NLAT, depth=2):
    T = NLAT + NCTX
    NKL = 4 * NLAT
    B = NLAT // 128
    HR = 32768 // NLAT
    R = 2048 + HR
    tiles = [(t0, 512, 0) for t0 in range(0, NLAT, 512)] + [(NLAT, 256, 1)]

    nc = bass.Bass("TRN2", target_bir_lowering=False)
    S = Sched(nc)
    ps_e = nc.tensor
    din = lambda name, shape: nc.dram_tensor(name, list(shape), F32, kind="ExternalInput").ap()
    xT_d = din("xT", [128, KC, T])
    tabs_d = din("tabs", [128, 2, T])
    vecs_d = din("vecs", [128, NV])
    W = []
    for l in range(depth):
        W.append(dict(
            wq=din(f"wq{l}", [16, 128, 8, 256]), wv=din(f"wv{l}", [2, 128, 8, 512]),
            wconv=din(f"wconv{l}", [8, 128, 8, 384]), wgt=din(f"wgt{l}", [8, 128, 8, 256]),
            wp=din(f"wp{l}", [3, 128, 8, 1024]), wff=din(f"wff{l}", [FC, 128, 8, 256]),
            wdn=din(f"wdn{l}", [8, 128, FC, 128]), wada=din(f"wada{l}", [12, 128, 8, 128])))
    yT_d = nc.dram_tensor("yT", [128, KC, NLAT], F32, kind="ExternalOutput").ap()
    NP_ = 4
    ownK_d = [[nc.dram_tensor(f"ownK{l}_{i}", [256, NLAT], BF16).ap() for i in range(NP_)] for l in range(depth)]
    gatK_d = [[nc.dram_tensor(f"gatK{l}_{i}", [1024, NLAT], BF16).ap() for i in range(NP_)] for l in range(depth)]
    ownV_d = [[nc.dram_tensor(f"ownV{l}_{i}", [256, NLAT], BF16).ap() for i in range(NP_)] for l in range(depth)]
    gatV_d = [[nc.dram_tensor(f"gatV{l}_{i}", [1024, NLAT], BF16).ap() for i in range(NP_)] for l in range(depth)]
    ownH_d = [nc.dram_tensor(f"ownH{l}", [HR, NLAT], BF16).ap() for l in range(depth)]
    gatH_d = [nc.dram_tensor(f"gatH{l}", [4 * HR, NLAT], BF16).ap() for l in range(depth)]
    ownM_d = nc.dram_tensor("ownM", [128, MW], F32).ap()
    gatM_d = nc.dram_tensor("gatM", [4 * 128, MW], F32).ap()
    QS = nc.dram_tensor("QS", [H, 128, T], BF16).ap()
    US = nc.dram_tensor("US", [8, 128, T], F32).ap()
    CBS = nc.dram_tensor("CBS", [8, 128, T], F32).ap()
    SGS = nc.dram_tensor("SGS", [8, 128, 2, T], F32).ap()
    ATS = nc.dram_tensor("ATS", [H, 128, T], BF16).ap()
    bQS, bUS, bCBS, bSGS, bATS = (Buf(n, multi=True) for n in ("QS", "US", "CBS", "SGS", "ATS"))
    bownK = [[Buf(f"ownK{l}{i}", multi=True) for i in range(4)] for l in range(depth)]
    bownV = [[Buf(f"ownV{l}{i}", multi=True) for i in range(4)] for l in range(depth)]
    bownH = [Buf(f"ownH{l}", multi=True) for l in range(depth)]
    bgatK = [[Buf(f"gatK{l}{i}") for i in range(4)] for l in range(depth)]
    bgatV = [[Buf(f"gatV{l}{i}") for i in range(4)] for l in range(depth)]
    bgatH = [Buf(f"gatH{l}") for l in range(depth)]

    sb = lambda name, shape, dt: nc.alloc_sbuf_tensor(name, list(shape), dt)
    _uid = [0]

    def sbt(name, shape, dt):
        _uid[0] += 1
        return nc.sbuf_tensor(f"{name}_u{_uid[0]}", list(shape), dt)
    xres = sb("xres", [128, KC, T], F32); bx = [[Buf(f"x{c}_{i}") for i in range(len(tiles))] for c in range(KC)]
    vec = sb("vec", [128, NV], F32); bvec = Buf("vec")
    onesD = sb("onesD", [128, 128], BF16)
    ones128 = sb("ones128", [128, 128], BF16)
    blk64 = sb("blk64", [128, 128], BF16)
    ones1 = sb("ones1", [128, 128], BF16)
    onesf = sb("onesf", [128, 128], F32)
    epsb = sb("epsb", [128, 1], F32)
    bconst = Buf("const")
    mods = sb("mods", [128, depth, 48, 2], F32); bmods = Buf("mods")
    a12 = sb("a12", [128, depth, 2, 8, 2], F32)
    lamv = sb("lamv", [128, depth], F32)
    sgl = sb("sgl", [128, depth], F32)
    ctxK = sb("ctxK", [128, H, NCTX], BF16); bctxK = Buf("ctxK", multi=True)
    ctxV = sb("ctxV", [128, 2, 1024], BF16); bctxV = Buf("ctxV", multi=True)
    halo = sb("halo", [128, 2, 8], F32); bhalo = Buf("halo")
    NWS = 3
    wsl = [None] * NWS
    bwsl = [Buf(f"wsl{i}") for i in range(NWS)]
    wctr = [0]

    def alloc_wsl(es):
        for i in range(NWS):
            wsl[i] = es.enter_context(sbt(f"wsl{i}", [128, 4096], BF16))
    ps = nc.alloc_psum_tensor("ps", [128, 8, 512], F32)
    bps = [Buf(f"ps{i}", excl=True) for i in range(8)]
    bkctr = [0]

    def nextbank():
        b = bkctr[0] % 8
        bkctr[0] += 1
        return b

    wplan = []
    wstate = {"next": 0, "issued": 0, "views": {}}

    def set_plan(items):
        wplan[:] = items
        wstate["next"] = 0
        wstate["issued"] = 0
        wstate["views"] = {}

    def _issue(k):
        src, kk, ncols = wplan[k]
        i = wctr[0] % NWS
        wctr[0] += 1
        view = wsl[i][:, 0:kk * ncols].rearrange("p (k n) -> p k n", k=kk)
        S.dma("pool", [(view, src)], writes=[bwsl[i]], key=f"wsl{i}")
        wstate["views"][k] = (view, bwsl[i])

    def load_w(src, kk, ncols):
        k = wstate["next"]
        wstate["next"] += 1
        assert wplan[k][1:] == (kk, ncols), (k, wplan[k][1:], kk, ncols)
        while wstate["issued"] < min(k + NWS, len(wplan)):
            _issue(wstate["issued"])
            wstate["issued"] += 1
        return wstate["views"].pop(k)

    class Pool_:
        def __init__(self, es, name, shape, dt, n):
            self.t = [es.enter_context(sbt(f"{name}{i}", list(shape), dt)) for i in range(n)]
            self.b = [Buf(f"{name}{i}") for i in range(n)]
            self.k = [f"{name}{i}" for i in range(n)]
            self.i = 0

        def next(self):
            j = self.i % len(self.t)
            self.i += 1
            return self.t[j], self.b[j], self.k[j]

    mm = lambda out, lhsT, rhs, st, sp: (lambda e: e.matmul(out, lhsT, rhs, start=st, stop=sp))

    S.op("dve", lambda e: e.memset(onesD[:], 1.0 / D), writes=[bconst])
    S.op("dve", lambda e: e.memset(ones128[:], 1.0 / 128), writes=[bconst])
    S.op("dve", lambda e: e.memset(ones1[:], 1.0), writes=[bconst])
    S.op("dve", lambda e: e.memset(onesf[:], 1.0), writes=[bconst])
    S.op("dve", lambda e: e.memset(epsb[:], EPS), writes=[bconst])
    S.op("dve", lambda e: e.memset(blk64[:], 0.0), writes=[bconst])
    S.op("dve", lambda e: e.memset(blk64[0:64, 0:64], 1.0 / 64), writes=[bconst])
    S.op("dve", lambda e: e.memset(blk64[64:128, 64:128], 1.0 / 64), writes=[bconst])
    S.dma("sp", [(vec[:], vecs_d)], writes=[bvec], key="vec")
    for c in range(KC):
        S.dma("sp", [(xres[:, c, :], xT_d[:, c, :])], writes=bx[c], key=f"xl{c}")

    GB = 2 * VL
    with ExitStack() as es:
        wad = es.enter_context(sbt("wad", [128, depth, 12, 8, 128], F32)); bwad = [Buf(f"wad{l}") for l in range(depth)]
        sc_t = es.enter_context(sbt("sc_t", [128, 8, 2], F32)); bsc = Buf("sc")
        tmp16 = es.enter_context(sbt("tmp16", [128, 16], F32)); btmp = Buf("tmp16")
        mown = es.enter_context(sbt("mown", [128, MW], F32)); bmown = Buf("mown")
        mgat = es.enter_context(sbt("mgat", [128, 4, MW], F32)); bmgat = Buf("mgat")
        bownM, bgatM = Buf("ownM"), Buf("gatM")
        S.op("act", lambda e: e.activation(out=sc_t[:].rearrange("p a b -> p (a b)"), in_=vec[:, GB:GB + 16],
                                           func=AF.Silu), reads=[bvec], writes=[bsc])
        for l in range(depth):
            S.dma("sp", [(wad[:, l, k], W[l]["wada"][k]) for k in range(12)], writes=[bwad[l]], key=f"wad{l}")
            for k in range(12):
                col = (l * 12 + k) * 2
                S.group([mm(ps[:, 0, col:col + 2], wad[:, l, k, kc, :], sc_t[:, kc, :], kc == 0, kc == KC - 1)
                         for kc in range(KC)], reads=[bwad[l], bsc], writes=[bps[0]])
        S.op("dve", lambda e: e.tensor_copy(mown[:], ps[:, 0, 0:MW]), reads=[bps[0]], writes=[bmown])
        S.dma("sp", [(ownM_d, mown[:])], reads=[bmown], writes=[bownM], key="ownM")
        S.collective([ownM_d.opt()], [gatM_d.opt()], [[0, 1, 2, 3], [4, 5, 6, 7]], reads=[bownM], writes=[bgatM])
        S.dma("sp", [(mgat[:], gatM_d.rearrange("(r p) x -> p r x", p=128))], reads=[bgatM], writes=[bmgat], key="mgat")
        mgv = mgat[:].rearrange("p r (l k c) -> p r l k c", l=depth, k=12)
        for l in range(depth):
            lam_init = 0.8 - 0.6 * math.exp(-0.3 * l)
            vb = l * VL
            for j in range(2):
                S.op("dve", lambda e, j=j: e.tensor_tensor(mods[:, l, :, j].rearrange("p (r k) -> p r k", k=12),
                                                           mgv[:, :, l, :, j],
                                                           vec[:, vb:vb + 48].rearrange("p (r k) -> p r k", k=12), ALU.add),
                     reads=[bmgat, bvec], writes=[bmods])
            for k2 in range(2):
                scv = 1 + 3 * k2
                S.op("dve", lambda e: e.tensor_scalar_add(
                    tmp16[:], mods[:, l, scv * 8:(scv + 1) * 8, :].rearrange("p a b -> p (a b)"), 1.0),
                    reads=[bmods], writes=[btmp])
                for j in range(2):
                    S.op("dve", lambda e, j=j: e.tensor_tensor(
                        a12[:, l, k2, :, j], tmp16[:].rearrange("p (a b) -> p a b", b=2)[:, :, j],
                        vec[:, vb + 48 + 8 * k2: vb + 56 + 8 * k2], ALU.mult),
                        reads=[btmp, bvec], writes=[bmods])
            lc = vb + 94
            S.op("dve", lambda e: e.tensor_tensor(tmp16[0:64, 0:1], vec[0:64, lc:lc + 1], vec[0:64, lc + 1:lc + 2],
                                                  ALU.mult), reads=[bvec, bmods], writes=[btmp])
            S.op("dve", lambda e: e.tensor_tensor(tmp16[0:64, 1:2], vec[0:64, lc + 2:lc + 3], vec[0:64, lc + 3:lc + 4],
                                                  ALU.mult), reads=[bvec], writes=[btmp])
            S.group([mm(ps[:, 2 + l, 0:2], onesf[0:64, :], tmp16[0:64, 0:2], True, True)],
                    reads=[btmp, bconst], writes=[bps[2 + l]])
            S.op("act", lambda e: e.activation(out=tmp16[:, 2:4], in_=ps[:, 2 + l, 0:2], func=AF.Exp),
                 reads=[bps[2 + l]], writes=[btmp])
            S.op("dve", lambda e: e.tensor_tensor(tmp16[:, 4:5], tmp16[:, 3:4], tmp16[:, 2:3], ALU.subtract),
                 reads=[btmp], writes=[btmp])
            S.op("dve", lambda e: e.tensor_scalar_add(lamv[:, l:l + 1], tmp16[:, 4:5], -lam_init),
                 reads=[btmp], writes=[bmods])
            S.op("dve", lambda e: e.tensor_scalar_mul(sgl[:, l:l + 1], vec[:, vb + 88:vb + 89], 1.0 - lam_init),
                 reads=[bvec], writes=[bmods])
        S.barrier()

    def MOD(l, v, c, j):
        return mods[:, l, v * 8 + c, j:j + 1]

    def norm_mod(es_pools, l, k2, dst, bdst, tl):
        sqx, rsP, tmpP = es_pools
        tl = list(tl)
        rts = {}

        def rstd_ops(ti):
            t0, n, j = tiles[ti]
            bk, rt, rb = rts[ti]
            S.op("act", lambda e: e.activation(out=rt[:, 0:n], in_=ps[:, bk, 0:n], func=AF.Ln, bias=epsb[:], scale=1.0),
                 reads=[bps[bk], bconst], writes=[rb])
            S.op("act", lambda e: e.activation(out=rt[:, 0:n], in_=rt[:, 0:n], func=AF.Exp, scale=-0.5),
                 reads=[rb], writes=[rb])

        prev_ti = None
        for ti in tl:
            t0, n, j = tiles[ti]
            st, sbf, _ = sqx.next()
            S.op("act", lambda e: e.activation(out=st[:, :, 0:n], in_=xres[:, :, t0:t0 + n], func=AF.Square),
                 reads=[bx[c][ti] for c in range(KC)], writes=[sbf])
            bk = nextbank()
            S.group([mm(ps[:, bk, 0:n], onesD[:], st[:, c, 0:n], c == 0, c == KC - 1) for c in range(KC)],
                    reads=[sbf, bconst], writes=[bps[bk]])
            rt, rb, _ = rsP.next()
            rts[ti] = (bk, rt, rb)
            if prev_ti is not None:
                rstd_ops(prev_ti)
            prev_ti = ti
        if prev_ti is not None:
            rstd_ops(prev_ti)
        for ti in tl:
            t0, n, j = tiles[ti]
            bk, rt, rb = rts[ti]
            for c in range(KC):
                tt, tb, _ = tmpP.next()
                S.op("dve", lambda e: e.tensor_tensor(tt[:, 0:n], xres[:, c, t0:t0 + n], rt[:, 0:n], ALU.mult),
                     reads=[bx[c][ti], rb], writes=[tb])
                S.op("act", lambda e: e.activation(out=dst[:, c, t0:t0 + n], in_=tt[:, 0:n], func=AF.Identity,
                                                   bias=MOD(l, 3 * k2, c, j), scale=a12[:, l, k2, c, j:j + 1]),
                     reads=[tb, bmods], writes=[bdst[c][ti]])

    groups = [[0, 1, 2, 3], [4, 5, 6, 7]]

    STOP = os.environ.get('KSTOP', '')
    for l in range(depth):
        last = l == depth - 1
        if STOP == 'p0':
            break
        vb = l * VL
        ntl = len(tiles) - 1 if last else len(tiles)
        ownVv = [ownV_d[l][i].rearrange("(h a) (b d) -> h (a b) d", h=2, d=128) for i in range(4)]
        ownH = ownH_d[l].rearrange("r c -> (r c)").rearrange("(s m p x) -> s m p x", s=2, m=8, p=128)

        with ExitStack() as es:
            hT = es.enter_context(sbt("hT", [128, KC, T], BF16))
            bh = [[Buf(f"h{c}_{i}") for i in range(len(tiles))] for c in range(KC)]
            tab = es.enter_context(sbt("tab", [128, 2, T], F32)); btab = Buf("tab")
            S.dma("sp", [(tab[:], tabs_d)], writes=[btab], key="tab")
            with ExitStack() as es2:
                sqx = Pool_(es2, "sqx", [128, KC, 512], BF16, 2)
                rsP = Pool_(es2, "rsP", [128, 512], F32, len(tiles))
                tmpP = Pool_(es2, "tmpP", [128, 512], F32, 3)
                norm_mod((sqx, rsP, tmpP), l, 0, hT, bh, range(len(tiles)))
                S.barrier()
            alloc_wsl(es)
            rsP = Pool_(es, "rsQ", [128, 512], F32, 2)
            sq1 = Pool_(es, "sq1", [128, 512], BF16, 2)
            t1P = Pool_(es, "t1P", [128, 512], F32, 2)
            t2P = Pool_(es, "t2P", [128, 512], F32, 2)
            qoP = Pool_(es, "qoP", [128, 512], BF16, 3)
            vsP = Pool_(es, "vsP", [128, 512], BF16, 3)
            f1P = Pool_(es, "f1P", [128, 512], F32, 3)
            f2P = Pool_(es, "f2P", [128, 512], F32, 3)
            sgP = Pool_(es, "sgP", [128, 2, 512], F32, 2)
            hsP = Pool_(es, "hsP", [128, 16], BF16, 4)
            print("phaseA sbuf remaining", nc.sbuf_bytes_remaining)

            def proj(wv_, wb_, col0, ti, bk):
                t0, n, j = tiles[ti]
                S.group([mm(ps[:, bk, 0:n], wv_[:, kc, col0:col0 + 128], hT[:, kc, t0:t0 + n], kc == 0, kc == KC - 1)
                         for kc in range(KC)], reads=[wb_] + [bh[kc][ti] for kc in range(KC)], writes=[bps[bk]])

            def qk_item(idx, is_k, h):
                wv_, wb_ = load_w(W[l]["wq"][idx], 8, 256)
                gcol = vb + (89 if is_k else 91)
                pend = None
                for ti in range(len(tiles)):
                    t0, n, j = tiles[ti]
                    if (not is_k) and last and j == 1:
                        continue
                    if os.environ.get('KNOCTX') and j == 1:
                        continue
                    if os.environ.get('KNOLAT') and j == 0:
                        continue
                    KQ = int(os.environ.get('KQ', '9'))
                    if KQ < 1:
                        continue
                    b0, b1 = nextbank(), nextbank()
                    proj(wv_, wb_, 0, ti, b0)
                    proj(wv_, wb_, 128, ti, b1)
                    if KQ < 2:
                        continue
                    s1, s1b, _ = sq1.next()
                    if not os.environ.get('KNOSQ'):
                        S.op("act", lambda e: e.activation(out=s1[:, 0:n], in_=ps[:, b0, 0:n], func=AF.Square),
                             reads=[bps[b0]], writes=[s1b])
                    t1, t1b, _ = t1P.next()
                    t2, t2b, _ = t2P.next()
                    S.op("dve", lambda e: e.scalar_tensor_tensor(t1[:, 0:n], ps[:, b0, 0:n], vec[:, gcol:gcol + 1],
                                                                 tab[:, 0, t0:t0 + n], ALU.mult, ALU.mult),
                         reads=[bps[b0], btab, bvec], writes=[t1b])
                    S.op("dve", lambda e: e.scalar_tensor_tensor(t2[:, 0:n], ps[:, b1, 0:n], vec[:, gcol + 1:gcol + 2],
                                                                 tab[:, 1, t0:t0 + n], ALU.mult, ALU.mult),
                         reads=[bps[b1], btab, bvec], writes=[t2b])

                    def fin(ti=ti, t0=t0, n=n, j=j, s1=s1, s1b=s1b, t1=t1, t1b=t1b, t2=t2, t2b=t2b):
                        bn = nextbank()
                        S.group([mm(ps[:, bn, 0:n], blk64[:], s1[:, 0:n], True, True)], reads=[s1b, bconst],
                                writes=[bps[bn]])
                        rt, rb, _ = rsP.next()
                        S.op("act", lambda e: e.activation(out=rt[:, 0:n], in_=ps[:, bn, 0:n], func=AF.Ln,
                                                           bias=epsb[:], scale=1.0), reads=[bps[bn], bconst], writes=[rb])
                        S.op("act", lambda e: e.activation(out=rt[:, 0:n], in_=rt[:, 0:n], func=AF.Exp, scale=-0.5),
                             reads=[rb], writes=[rb])
                        S.op("dve", lambda e: e.tensor_tensor(t1[:, 0:n], t1[:, 0:n], t2[:, 0:n], ALU.add),
                             reads=[t1b, t2b], writes=[t1b])
                        if is_k and j == 1:
                            S.op("dve", lambda e: e.tensor_tensor(ctxK[:, h, :], t1[:, 0:n], rt[:, 0:n], ALU.mult),
                                 reads=[t1b, rb], writes=[bctxK])
                        else:
                            qo, qob, qk_ = qoP.next()
                            S.op("dve", lambda e: e.tensor_tensor(qo[:, 0:n], t1[:, 0:n], rt[:, 0:n], ALU.mult),
                                 reads=[t1b, rb], writes=[qob])
                            if is_k:
                                S.dma("sp", [(ownK_d[l][h // 2][(h % 2) * 128:(h % 2 + 1) * 128, t0:t0 + n], qo[:, 0:n])],
                                      reads=[qob], writes=[bownK[l][h // 2]], key=qk_)
                            else:
                                S.dma("sp", [(QS[h][:, t0:t0 + n], qo[:, 0:n])], reads=[qob], writes=[bQS], key=qk_)
                    if KQ < 3:
                        continue
                    if pend:
                        pend()
                    pend = fin
                if pend:
                    pend()

            def v_item(g):
                wv_, wb_ = load_w(W[l]["wv"][g], 8, 512)
                for ti in range(len(tiles)):
                    t0, n, j = tiles[ti]
                    for s_ in range(n // 128):
                        bk = nextbank()
                        c0 = t0 + s_ * 128
                        S.group([mm(ps[:, bk, :], hT[:, kc, c0:c0 + 128], wv_[:, kc, :], kc == 0, kc == KC - 1)
                                 for kc in range(KC)], reads=[wb_] + [bh[kc][ti] for kc in range(KC)],
                                writes=[bps[bk]])
                        if j == 1:
                            S.op("act", lambda e: e.copy(ctxV[:, s_, g * 512:(g + 1) * 512], ps[:, bk, :]),
                                 reads=[bps[bk]], writes=[bctxV])
                        else:
                            vs, vsb, vk_ = vsP.next()
                            S.op("act", lambda e: e.copy(vs[:], ps[:, bk, :]), reads=[bps[bk]], writes=[vsb])
                            S.dma("sp", [(ownVv[2 * g + k2][:, c0:c0 + 128, :].rearrange("h t d -> t h d"),
                                          vs[:, 256 * k2:256 * (k2 + 1)].rearrange("p (h d) -> p h d", h=2))
                                         for k2 in range(2)], reads=[vsb],
                                  writes=[bownV[l][2 * g], bownV[l][2 * g + 1]], key=vk_)

            def conv_item(m):
                wv_, wb_ = load_w(W[l]["wconv"][m], 8, 384)
                for ti in range(ntl):
                    t0, n, j = tiles[ti]
                    b0, b1, b2 = nextbank(), nextbank(), nextbank()
                    proj(wv_, wb_, 0, ti, b0)
                    proj(wv_, wb_, 128, ti, b1)
                    proj(wv_, wb_, 256, ti, b2)
                    f1, f1b, f1k = f1P.next()
                    f2, f2b, f2k = f2P.next()
                    S.op("act", lambda e: e.copy(f1[:, 0:n], ps[:, b0, 0:n]), reads=[bps[b0]], writes=[f1b])
                    S.op("dve", lambda e: e.tensor_tensor(f1[:, 0:n], ps[:, b1, 0:n], f1[:, 0:n], ALU.mult),
                         reads=[bps[b1], f1b], writes=[f1b])
                    S.dma("sp", [(US[m][:, t0:t0 + n], f1[:, 0:n])], reads=[f1b], writes=[bUS], key=f1k)
                    S.op("act", lambda e: e.copy(f2[:, 0:n], ps[:, b2, 0:n]), reads=[bps[b2]], writes=[f2b])
                    S.dma("sp", [(CBS[m][:, t0:t0 + n], f2[:, 0:n])], reads=[f2b], writes=[bCBS], key=f2k)
                    if j == 0 and t0 == 0:
                        hs, hsb, hk_ = hsP.next()
                        S.op("dve", lambda e: e.tensor_copy(hs[:], f1[:, 0:16]), reads=[f1b], writes=[hsb])
                        S.dma("sp", [(ownH[0, m], hs[:])], reads=[hsb], writes=[bownH[l]], key=hk_)
                    if j == 0 and t0 + n == NLAT:
                        hs, hsb, hk_ = hsP.next()
                        S.op("dve", lambda e: e.tensor_copy(hs[:], f1[:, n - 16:n]), reads=[f1b], writes=[hsb])
                        S.dma("sp", [(ownH[1, m], hs[:])], reads=[hsb], writes=[bownH[l]], key=hk_)

            def gate_item(m):
                wv_, wb_ = load_w(W[l]["wgt"][m], 8, 256)
                for ti in range(ntl):
                    t0, n, j = tiles[ti]
                    b0, b1 = nextbank(), nextbank()
                    proj(wv_, wb_, 0, ti, b0)
                    proj(wv_, wb_, 128, ti, b1)
                    sg, sgb, sgk = sgP.next()
                    S.op("act", lambda e: e.activation(out=sg[:, 0, 0:n], in_=ps[:, b0, 0:n], func=AF.Sigmoid),
                         reads=[bps[b0]], writes=[sgb])
                    S.op("act", lambda e: e.activation(out=sg[:, 1, 0:n], in_=ps[:, b1, 0:n], func=AF.Sigmoid),
                         reads=[bps[b1]], writes=[sgb])
                    S.dma("sp", [(SGS[m][:, :, t0:t0 + n], sg[:, :, 0:n])], reads=[sgb], writes=[bSGS], key=sgk)

            set_plan([(W[l]["wq"][i], 8, 256) for i in range(8)] + [(W[l]["wv"][g], 8, 512) for g in range(2)]
                     + [(W[l]["wconv"][m], 8, 384) for m in range(8)] + [(W[l]["wq"][8 + i], 8, 256) for i in range(8)]
                     + [(W[l]["wgt"][m], 8, 256) for m in range(8)])
            for h in range(int(os.environ.get('KITEMS', H))):
                if STOP != 'A0':
                    qk_item(h, True, h)
                    if h % 2 == 1 and STOP not in ('A1', 'A2', 'A3'):
                        S.collective([ownK_d[l][h // 2].opt()], [gatK_d[l][h // 2].opt()], groups,
                                     reads=[bownK[l][h // 2]], writes=[bgatK[l][h // 2]])
            for g in range(2):
                if STOP not in ('A0', 'A1'):
                    v_item(g)
                    if STOP not in ('A2', 'A3'):
                        for i in (2 * g, 2 * g + 1):
                            S.collective([ownV_d[l][i].opt()], [gatV_d[l][i].opt()], groups,
                                         reads=[bownV[l][i]], writes=[bgatV[l][i]])
            for m in range(8):
                if STOP not in ('A0', 'A1', 'A2'):
                    conv_item(m)
            if STOP not in ('A0', 'A1', 'A2', 'A3'):
                S.collective([ownH_d[l].opt()], [gatH_d[l].opt()], groups, reads=[bownH[l]], writes=[bgatH[l]])
            for h in range(H):
                if STOP not in ('A0', 'A1', 'A2', 'A3', 'A4'):
                    qk_item(8 + h, False, h)
            for m in range(8):
                if STOP not in ('A0', 'A1', 'A2', 'A3', 'A4', 'A5'):
                    gate_item(m)
            S.barrier()

        if STOP.startswith('A'):
            break
        with ExitStack() as es:
            khP = Pool_(es, "kh", [128, NKL], BF16, 2)
            vhP = Pool_(es, "vh", [128, NKL // 128, 128], BF16, 2)
            qhP = Pool_(es, "qh", [128, T], BF16, 2)
            NPB = 3
            pbT = [es.enter_context(sbt(f"pb{i}", [128, 2, 2, 512], BF16)) for i in range(NPB)]
            pbB = [[Buf(f"pb{i}_{k}") for k in range(2)] for i in range(NPB)]
            pbctr = [0]
            rzP = Pool_(es, "rz", [128, 2, 512], F32, 1)
            o1P = Pool_(es, "o1", [128, 512], F32, 1)
            o2P = Pool_(es, "o2", [128, 512], F32, 1)
            osP = Pool_(es, "osq", [128, 512], BF16, 1)
            rnP = Pool_(es, "rn", [128, 512], F32, 1)
            atP = Pool_(es, "at", [128, 512], BF16, 2)
            paccP = Pool_(es, "pacc", [128, 3, 512], F32, 2)
            zoP = Pool_(es, "zo", [128, 512], F32, 1)
            print("phaseB1 sbuf remaining", nc.sbuf_bytes_remaining)

            def load_head(h):
                kt, kb, kk_ = khP.next()
                vt, vb_, vk_ = vhP.next()
                qt, qb, qk_ = qhP.next()
                S.dma("sp", [(qt[:], QS[h])], reads=[bQS], writes=[qb], key=qk_)
                hp, ho = h // 2, (h % 2) * 128
                S.dma("sp", [(kt[:, r * NLAT:(r + 1) * NLAT], gatK_d[l][hp][r * 256 + ho: r * 256 + ho + 128, :])
                             for r in range(4)], reads=[bgatK[l][hp]], writes=[kb], key=kk_)
                pairs = []
                for r in range(4):
                    src = gatV_d[l][hp][r * 256 + ho: r * 256 + ho + 128, :]
                    src = src.rearrange("a (b d) -> (a b) d", d=128).rearrange("(c p) d -> p c d", p=128)
                    pairs.append((vt[:, r * B:(r + 1) * B, :], src))
                S.dma("sp", pairs, reads=[bgatV[l][hp]], writes=[vb_], key=vk_)
                return (kt, kb, vt, vb_, qt, qb)

            pending = []
            nxt = load_head(0)
            for h in range(H):
                kt, kb, vt, vb_, qt, qb = nxt
                if h + 1 < H:
                    nxt = load_head(h + 1)
                for ti in range(ntl):
                    t0, n, j = tiles[ti]
                    nk = 2 if j == 1 else 2 + NKL // 128

                    def kv(c):
                        if c < 2:
                            return ctxK[:, h, c * 128:(c + 1) * 128], ctxV[:, c, h * 128:(h + 1) * 128], [bctxK, bctxV]
                        return kt[:, (c - 2) * 128:(c - 1) * 128], vt[:, c - 2, :], [kb, vb_]

                    def s_mm(c):
                        kk, _, bb = kv(c)
                        s0 = 2 * (c % 2)
                        S.group([mm(ps[:, s0, 0:n], kk[0:64, :], qt[0:64, t0:t0 + n], True, True),
                                 mm(ps[:, s0 + 1, 0:n], kk[64:128, :], qt[64:128, t0:t0 + n], True, True)],
                                reads=bb + [qb], writes=[bps[s0], bps[s0 + 1]])

                    def e_op(c):
                        s0 = 2 * (c % 2)
                        if c % 2 == 0:
                            pbctr[0] += 1
                        i = pbctr[0] % NPB
                        pt, pbb = pbT[i], pbB[i][c % 2]
                        S.op("act", lambda e: e.activation(out=pt[:, c % 2, :, 0:n], in_=ps[:, s0:s0 + 2, 0:n], func=AF.Exp,
                                                           scale=ATTN_SCALE), reads=[bps[s0], bps[s0 + 1]], writes=[pbb])
                        return pt, pbb, i

                    def pv_mm(c, pt, pbb, i):
                        _, vv, bb = kv(c)
                        st_, sp_ = c == 0, c == nk - 1
                        fns = [mm(ps[:, 4, 0:n], vv, pt[:, c % 2, 0, 0:n], st_, sp_),
                               mm(ps[:, 5, 0:n], vv, pt[:, c % 2, 1, 0:n], st_, sp_)]
                        wr = [bps[4], bps[5]]
                        if c % 2 == 1:
                            fns.append(mm(ps[:, 7, 0:n], ones1[:], pt[:, 1, 1, 0:n], c == 1, c == nk - 1))
                            wr.append(bps[7])
                        S.group(fns, reads=bb + [pbb, bconst], writes=wr)

                    pacc, paccb, _ = paccP.next()

                    def z_acc(c, pt, pbb, i):
                        if c % 2 == 0:
                            return
                        p3 = pt[:].rearrange("p a b q -> p (a b) q")[:, 0:3, 0:n]
                        if c == 1:
                            S.op("dve", lambda e: e.tensor_copy(pacc[:, :, 0:n], p3), reads=pbB[i], writes=[paccb])
                        else:
                            S.op("dve", lambda e: e.tensor_tensor(pacc[:, :, 0:n], pacc[:, :, 0:n], p3, ALU.add),
                                 reads=pbB[i] + [paccb], writes=[paccb])

                    s_mm(0)
                    if nk > 1:
                        s_mm(1)
                    for c in range(nk):
                        cur = e_op(c)
                        z_acc(c, *cur)
                        if c + 2 < nk:
                            s_mm(c + 2)
                        pv_mm(c, *cur)
                        while pending and pending[0][0] <= c:
                            pending.pop(0)[1]()
                    while pending:
                        pending.pop(0)[1]()
                    o1, o1b, _ = o1P.next(); o2, o2b, _ = o2P.next()
                    rz, rzb, _ = rzP.next()
                    rn, rnb, _ = rnP.next()
                    osq, osb, _ = osP.next()
                    zo, zob, _ = zoP.next()
                    S.op("dve", lambda e: e.tensor_copy(o1[:, 0:n], ps[:, 4, 0:n]), reads=[bps[4]], writes=[o1b])
                    S.op("dve", lambda e: e.tensor_copy(o2[:, 0:n], ps[:, 5, 0:n]), reads=[bps[5]], writes=[o2b])
                    S.op("dve", lambda e: e.tensor_copy(zo[:, 0:n], ps[:, 7, 0:n]), reads=[bps[7]], writes=[zob])

                    def e2a(n=n, pacc=pacc, paccb=paccb):
                        S.group([mm(ps[:, 6, 0:n], onesf[:], pacc[:, 0, 0:n], True, False),
                                 mm(ps[:, 6, 0:n], onesf[:], pacc[:, 2, 0:n], False, True)],
                                reads=[paccb, bconst], writes=[bps[6]])

                    def e3a(n=n, rz=rz, rzb=rzb):
                        S.op("act", lambda e: e.activation(out=rz[:, 0, 0:n], in_=ps[:, 6, 0:n], func=AF.Ln),
                             reads=[bps[6]], writes=[rzb])

                    def e2b(n=n, pacc=pacc, paccb=paccb):
                        S.group([mm(ps[:, 6, 0:n], onesf[:], pacc[:, 1, 0:n], True, True)],
                                reads=[paccb, bconst], writes=[bps[6]])

                    def e34(n=n, o1=o1, o1b=o1b, o2=o2, o2b=o2b, rz=rz, rzb=rzb, osq=osq, osb=osb, zo=zo, zob=zob):
                        S.op("dve", lambda e: e.tensor_tensor(rz[:, 1, 0:n], ps[:, 6, 0:n], zo[:, 0:n], ALU.add),
                             reads=[bps[6], zob, rzb], writes=[rzb])
                        S.op("act", lambda e: e.activation(out=rz[:, 1, 0:n], in_=rz[:, 1, 0:n], func=AF.Ln),
                             reads=[rzb], writes=[rzb])
                        S.op("act", lambda e: e.activation(out=rz[:, :, 0:n], in_=rz[:, :, 0:n], func=AF.Exp, scale=-1.0),
                             reads=[rzb], writes=[rzb])
                        S.op("dve", lambda e: e.tensor_tensor(o1[:, 0:n], o1[:, 0:n], rz[:, 0, 0:n], ALU.mult),
                             reads=[o1b, rzb], writes=[o1b])
                        S.op("dve", lambda e: e.tensor_tensor(o2[:, 0:n], o2[:, 0:n], rz[:, 1, 0:n], ALU.mult),
                             reads=[o2b, rzb], writes=[o2b])
                        S.op("dve", lambda e: e.scalar_tensor_tensor(o1[:, 0:n], o2[:, 0:n], lamv[:, l:l + 1], o1[:, 0:n],
                                                                     ALU.mult, ALU.add), reads=[o2b, o1b, bmods], writes=[o1b])
                        S.op("dve", lambda e: e.tensor_tensor(osq[:, 0:n], o1[:, 0:n], o1[:, 0:n], ALU.mult),
                             reads=[o1b], writes=[osb])

                    def e5(n=n, osq=osq, osb=osb):
                        S.group([mm(ps[:, 6, 0:n], ones128[:], osq[:, 0:n], True, True)], reads=[osb, bconst],
                                writes=[bps[6]])

                    def e67(n=n, t0=t0, h=h, o1=o1, o1b=o1b, rn=rn, rnb=rnb):
                        S.op("act", lambda e: e.activation(out=rn[:, 0:n], in_=ps[:, 6, 0:n], func=AF.Ln, bias=epsb[:],
                                                           scale=1.0), reads=[bps[6], bconst], writes=[rnb])
                        S.op("act", lambda e: e.activation(out=rn[:, 0:n], in_=rn[:, 0:n], func=AF.Exp, scale=-0.5),
                             reads=[rnb], writes=[rnb])
                        at, atb, atk = atP.next()
                        S.op("dve", lambda e: e.scalar_tensor_tensor(at[:, 0:n], o1[:, 0:n], sgl[:, l:l + 1], rn[:, 0:n],
                                                                     ALU.mult, ALU.mult), reads=[o1b, rnb, bmods], writes=[atb])
                        S.dma("sp", [(ATS[h][:, t0:t0 + n], at[:, 0:n])], reads=[atb], writes=[bATS], key=atk)

                    pending.extend([(1, e2a), (2, e3a), (3, e2b), (4, e34), (6, e5), (8, e67)])
            while pending:
                pending.pop(0)[1]()
            S.barrier()

        if STOP == 'B1':
            break
        with ExitStack() as es:
            wres = [es.enter_context(sbt(f"wres{i}", [128, 8, 1024], BF16)) for i in range(3)]
            bwres = [Buf(f"wres{i}") for i in range(3)]
            attP = Pool_(es, "att", [128, 8, 512], BF16, 2)
            ycP = Pool_(es, "yc", [128, 8, 512], BF16, 2)
            mgP = Pool_(es, "mg", [128, 8, 512], BF16, 1)
            ueP = Pool_(es, "ue", [128, 514], F32, 2)
            cbP = Pool_(es, "cbt", [128, 512], F32, 2)
            c1P = Pool_(es, "c1", [128, 512], F32, 2)
            sgP2 = Pool_(es, "sg2", [128, 2, 512], F32, 2)
            m1P = Pool_(es, "m1", [128, 512], F32, 2)
            m2P = Pool_(es, "m2", [128, 512], F32, 2)
            hl = es.enter_context(sbt("hl", [128, 4, 2, 8, 16], BF16)); bhl = Buf("hl")
            print("phaseB2 sbuf remaining", nc.sbuf_bytes_remaining)
            bwres = [[Buf(f"wres{i}_{m}") for m in range(8)] for i in range(3)]
            for i in (0, 1, 2):
                for m in range(8):
                    S.dma("pool", [(wres[i][:, :, m * 128:(m + 1) * 128], W[l]["wp"][i][:, :, m * 128:(m + 1) * 128])],
                          writes=[bwres[i][m]], key=f"wres{i}_{m}")
            pairs = []
            for r in range(4):
                src = gatH_d[l][r * HR: (r + 1) * HR, :].rearrange("r c -> (r c)").rearrange(
                    "(s m p x) -> p s m x", s=2, m=8, p=128)
                for s_ in range(2):
                    pairs.append((hl[:, r, s_], src[:, s_]))
            S.dma("sp", pairs, reads=[bgatH[l]], writes=[bhl], key="hl")
            for s_ in range(2):
                mc = GB + 24 + 4 * s_
                for r in range(4):
                    src = hl[:, r, 1 - s_, :, 15 if s_ == 0 else 0]
                    if r == 0:
                        S.op("dve", lambda e: e.tensor_scalar_mul(halo[:, s_, :], src, vec[:, mc:mc + 1]),
                             reads=[bhl, bvec], writes=[bhalo])
                    else:
                        S.op("dve", lambda e: e.scalar_tensor_tensor(halo[:, s_, :], src, vec[:, mc + r:mc + r + 1],
                                                                     halo[:, s_, :], ALU.mult, ALU.add),
                             reads=[bhl, bvec, bhalo], writes=[bhalo])
            def b2_prep(ti):
                t0, n, j = tiles[ti]
                att, attb, attk = attP.next()
                S.dma("sp", [(att[:, :, 0:n], ATS[:, :, t0:t0 + n].rearrange("h p t -> p h t"))], reads=[bATS],
                      writes=[attb], key=attk)
                yc, ycb, _ = ycP.next()
                lo_edge = (j == 1) or t0 == 0
                hi_edge = (j == 1) or (t0 + n == NLAT)
                for m in range(8):
                    ue, ueb, uek = ueP.next()
                    a0 = t0 if lo_edge else t0 - 1
                    a1 = t0 + n if hi_edge else t0 + n + 1
                    S.dma("sp", [(ue[:, 1 + a0 - t0: 1 + a1 - t0], US[m][:, a0:a1])], reads=[bUS], writes=[ueb], key=uek)
                    if lo_edge:
                        if j == 1:
                            S.op("dve", lambda e: e.memset(ue[:, 0:1], 0.0), writes=[ueb])
                        else:
                            S.op("dve", lambda e: e.tensor_copy(ue[:, 0:1], halo[:, 0, m:m + 1]), reads=[bhalo],
                                 writes=[ueb])
                    if hi_edge:
                        if j == 1:
                            S.op("dve", lambda e: e.memset(ue[:, n + 1:n + 2], 0.0), writes=[ueb])
                        else:
                            S.op("dve", lambda e: e.tensor_copy(ue[:, n + 1:n + 2], halo[:, 1, m:m + 1]),
                                 reads=[bhalo], writes=[ueb])
                    cbt, cbb, cbk = cbP.next()
                    S.dma("sp", [(cbt[:, 0:n], CBS[m][:, t0:t0 + n])], reads=[bCBS], writes=[cbb], key=cbk)
                    c1, c1b, _ = c1P.next()
                    cw = vb + 64
                    S.op("dve", lambda e: e.tensor_scalar_mul(c1[:, 0:n], ue[:, 1:n + 1], vec[:, cw + 8 + m:cw + 9 + m]),
                         reads=[ueb, bvec], writes=[c1b])
                    S.op("dve", lambda e: e.scalar_tensor_tensor(c1[:, 0:n], ue[:, 0:n], vec[:, cw + m:cw + m + 1],
                                                                 c1[:, 0:n], ALU.mult, ALU.add),
                         reads=[ueb, bvec, c1b], writes=[c1b])
                    S.op("dve", lambda e: e.scalar_tensor_tensor(c1[:, 0:n], ue[:, 2:n + 2],
                                                                  vec[:, cw + 16 + m:cw + 17 + m], c1[:, 0:n],
                                                                  ALU.mult, ALU.add), reads=[ueb, bvec, c1b], writes=[c1b])
                    S.op("dve", lambda e: e.tensor_tensor(yc[:, m, 0:n], c1[:, 0:n], cbt[:, 0:n], ALU.mult),
                         reads=[c1b, cbb], writes=[ycb])
                return att, attb, yc, ycb

            def b2_compute(ti, att, attb, yc, ycb):
                t0, n, j = tiles[ti]
                mg, mgb, _ = mgP.next()
                for m in range(8):
                    b0, b1 = nextbank(), nextbank()
                    S.group([mm(ps[:, b0, 0:n], wres[0][:, k, m * 128:(m + 1) * 128], att[:, k, 0:n], k == 0, k == 7)
                             for k in range(8)], reads=[bwres[0][m], attb], writes=[bps[b0]])
                    S.group([mm(ps[:, b1, 0:n], wres[1][:, k, m * 128:(m + 1) * 128], yc[:, k, 0:n], k == 0, k == 7)
                             for k in range(8)], reads=[bwres[1][m], ycb], writes=[bps[b1]])
                    sg, sgb, sgk = sgP2.next()
                    S.dma("sp", [(sg[:, :, 0:n], SGS[m][:, :, t0:t0 + n])], reads=[bSGS], writes=[sgb], key=sgk)
                    m1, m1b, _ = m1P.next(); m2, m2b, _ = m2P.next()
                    S.op("dve", lambda e: e.tensor_tensor(m1[:, 0:n], ps[:, b0, 0:n], sg[:, 0, 0:n], ALU.mult),
                         reads=[bps[b0], sgb], writes=[m1b])
                    S.op("dve", lambda e: e.tensor_tensor(m2[:, 0:n], ps[:, b1, 0:n], sg[:, 1, 0:n], ALU.mult),
                         reads=[bps[b1], sgb], writes=[m2b])
                    S.op("dve", lambda e: e.tensor_tensor(mg[:, m, 0:n], m1[:, 0:n], m2[:, 0:n], ALU.add),
                         reads=[m1b, m2b], writes=[mgb])
                for m in range(8):
                    b0 = nextbank()
                    S.group([mm(ps[:, b0, 0:n], wres[2][:, k, m * 128:(m + 1) * 128], mg[:, k, 0:n], k == 0, k == 7)
                             for k in range(8)], reads=[bwres[2][m], mgb], writes=[bps[b0]])
                    S.op("dve", lambda e: e.scalar_tensor_tensor(xres[:, m, t0:t0 + n], ps[:, b0, 0:n], MOD(l, 2, m, j),
                                                                 xres[:, m, t0:t0 + n], ALU.mult, ALU.add),
                         reads=[bps[b0], bmods, bx[m][ti]], writes=[bx[m][ti]])
            nxt_p = b2_prep(0)
            for ti in range(ntl):
                cur_p = nxt_p
                if ti + 1 < ntl:
                    nxt_p = b2_prep(ti + 1)
                b2_compute(ti, *cur_p)
            S.barrier()

        if STOP == 'B2':
            break
        with ExitStack() as es:
            sbs, curb, wsum = [], [], 0
            for ti in range(ntl):
                if wsum + tiles[ti][1] > 1280:
                    sbs.append(curb); curb, wsum = [], 0
                curb.append(ti); wsum += tiles[ti][1]
            sbs.append(curb)
            hf = es.enter_context(sbt("hf", [128, KC, T], BF16))
            bhf = [[Buf(f"hf{c}_{i}") for i in range(len(tiles))] for c in range(KC)]
            act = es.enter_context(sbt("actb", [128, FC, 1280], BF16))
            bact = [[Buf(f"act{f}_{i}") for i in range(len(tiles))] for f in range(FC)]
            with ExitStack() as es2:
                sqx = Pool_(es2, "sqx", [128, KC, 512], BF16, 2)
                rsP = Pool_(es2, "rsP", [128, 512], F32, len(tiles))
                tmpP = Pool_(es2, "tmpP", [128, 512], F32, 3)
                norm_mod((sqx, rsP, tmpP), l, 1, hf, bhf, range(ntl))
                S.barrier()
            alloc_wsl(es)
            siP = Pool_(es, "si", [128, 512], F32, 2)
            print("phaseB3 sbuf remaining", nc.sbuf_bytes_remaining)
            for sblk in sbs:
                offs, o = {}, 0
                for ti in sblk:
                    offs[ti] = o
                    o += tiles[ti][1]
                set_plan([(W[l]["wff"][f], 8, 256) for f in range(FC)] + [(W[l]["wdn"][m], FC, 128) for m in range(8)])
                for f in range(FC):
                    wv_, wb_ = load_w(W[l]["wff"][f], 8, 256)
                    for ti in sblk:
                        t0, n, j = tiles[ti]
                        b0, b1 = nextbank(), nextbank()
                        for bk, c0 in ((b0, 0), (b1, 128)):
                            S.group([mm(ps[:, bk, 0:n], wv_[:, kc, c0:c0 + 128], hf[:, kc, t0:t0 + n], kc == 0, kc == KC - 1)
                                     for kc in range(KC)], reads=[wb_] + [bhf[kc][ti] for kc in range(KC)],
                                    writes=[bps[bk]])
                        si, sib, _ = siP.next()
                        S.op("act", lambda e: e.activation(out=si[:, 0:n], in_=ps[:, b0, 0:n], func=AF.Silu),
                             reads=[bps[b0]], writes=[sib])
                        S.op("dve", lambda e: e.tensor_tensor(act[:, f, offs[ti]:offs[ti] + n], ps[:, b1, 0:n], si[:, 0:n],
                                                              ALU.mult), reads=[bps[b1], sib], writes=[bact[f][ti]])
                for m in range(8):
                    wv_, wb_ = load_w(W[l]["wdn"][m], FC, 128)
                    for ti in sblk:
                        t0, n, j = tiles[ti]
                        b0 = nextbank()
                        S.group([mm(ps[:, b0, 0:n], wv_[:, f, :], act[:, f, offs[ti]:offs[ti] + n], f == 0, f == FC - 1)
                                 for f in range(FC)], reads=[wb_] + [bact[f][ti] for f in range(FC)], writes=[bps[b0]])
                        S.op("dve", lambda e: e.scalar_tensor_tensor(xres[:, m, t0:t0 + n], ps[:, b0, 0:n],
                                                                     MOD(l, 5, m, j), xres[:, m, t0:t0 + n],
                                                                     ALU.mult, ALU.add),
                             reads=[bps[b0], bmods, bx[m][ti]], writes=[bx[m][ti]])
            S.barrier()
        if STOP == 'L0':
            break

    bout = Buf("out")
    S.dma("sp", [(yT_d[:, c, :], xres[:, c, 0:NLAT]) for c in range(KC)],
          reads=[bx[c][ti] for c in range(KC) for ti in range(len(tiles) - 1)], writes=[bout], key="out")
    S._waits("sp", [bout], [])
    return nc


_CACHE = {}


def kernel(x, c, ctx, c_ctx, w_ada, b_ada, norm1_g, norm2_g, w_in, q_norm_g, k_norm_g,
           lambda_q1, lambda_k1, lambda_q2, lambda_k2, subln_g, conv_w, w_pa, w_pc, w_o,
           w_ffn_gate, w_ffn_up, w_ffn_down):
    f = lambda a: np.asarray(a, np.float32)
    x, c, ctx, c_ctx = f(x), f(c), f(ctx), f(c_ctx)
    Bn, SEQ, _ = x.shape
    depth = w_in.shape[0]
    NLAT = SEQ // 4
    T = NLAT + NCTX
    if NLAT not in _CACHE:
        _CACHE[NLAT] = build(NLAT, depth)
    nc = _CACHE[NLAT]
    shared = {}
    for l in range(depth):
        lw = _layer_weights(f(w_in[l]), f(w_pa[l]), f(w_pc[l]), f(w_o[l]), f(w_ffn_gate[l]), f(w_ffn_up[l]),
                            f(w_ffn_down[l]))
        for k, v in lw.items():
            shared[f"{k}{l}"] = v
    p = np.arange(128)
    in_maps = []
    for core in range(8):
        b, j = core // 4, core % 4
        xs = np.concatenate([x[b, j * NLAT:(j + 1) * NLAT], ctx[b]], axis=0)
        xT = np.ascontiguousarray(xs.reshape(T, KC, 128).transpose(2, 1, 0))
        vecs = np.zeros((128, NV), np.float32)
        for l in range(depth):
            vb = l * VL
            vecs[:, vb:vb + 48] = _fm(b_ada[l])
            vecs[:, vb + 48:vb + 56] = _fm(norm1_g[l])
            vecs[:, vb + 56:vb + 64] = _fm(norm2_g[l])
            cw = f(conv_w[l])
            for k in range(3):
                vecs[:, vb + 64 + 8 * k: vb + 72 + 8 * k] = _fm(cw[k])
            vecs[:, vb + 88] = f(subln_g[l])
            gk, gq = f(k_norm_g[l]), f(q_norm_g[l])
            vecs[:, vb + 89] = gk[p % 64]
            vecs[:, vb + 90] = gk[((p % 64) + 32) % 64]
            vecs[:, vb + 91] = gq[p % 64]
            vecs[:, vb + 92] = gq[((p % 64) + 32) % 64]
            for i, lv in enumerate((lambda_q1, lambda_k1, lambda_q2, lambda_k2)):
                vecs[0:64, vb + 94 + i] = f(lv[l])
        GB = 2 * VL
        cT = np.stack([_fm(c[b]), _fm(c_ctx)], axis=-1)
        vecs[:, GB:GB + 16] = cT.reshape(128, 16)
        if j > 0:
            vecs[:, GB + 24 + (j - 1)] = 1.0
        if j < 3:
            vecs[:, GB + 28 + (j + 1)] = 1.0
        m = {"xT": xT, "tabs": _rope_tabs(j * NLAT, NLAT), "vecs": vecs}
        for l in range(depth):
            wa = f(w_ada[l])
            m[f"wada{l}"] = np.ascontiguousarray(np.stack(
                [_wch(wa, 128 * (j * 12 + k) + np.arange(128)) for k in range(12)]))
        m.update(shared)
        in_maps.append(m)
    res = run_bass_kernel_spmd(nc, in_maps, core_ids=list(range(8)))
    out = np.zeros((Bn, SEQ, D), np.float32)
    for core in range(8):
        b, j = core // 4, core % 4
        yT = np.asarray(res.results[core]["yT"], np.float32)
        out[b, j * NLAT:(j + 1) * NLAT] = yT.transpose(2, 1, 0).reshape(NLAT, D)
    return out
```
